# Optimizing a Trainium2 kernel written in Bass

```python
import jax, jax.numpy as jnp
from jax import lax
import numpy as np

D_MODEL = 2048
BATCH = 4
SEQ = 8192
DEPTH = 4

CHUNK = 64
D_MIX = D_MODEL
ATTN_HEAD_DIM = 128
ATTN_HEADS = (D_MIX // 2) // ATTN_HEAD_DIM
D_ATTN = ATTN_HEADS * ATTN_HEAD_DIM
D_CONF = D_MIX // 4
D_SCONV = D_MIX - D_ATTN - D_CONF
CONF_WIDTH = 31
SCONV_WIDTH = 3
FFN_WIDTH = 3
D_FF = 5632
Q_BLOCK = 128
N_ADA = 6
RMS_EPS = 1e-6
LN_EPS = 1e-5
IN_COLS = 3 * D_ATTN + ATTN_HEADS + 2 * D_CONF + 3 * D_SCONV

kernel_name = "hybrid_fox_conformer_shortconv_trunk"


def rms_norm(x, g):
    xf = x.astype(jnp.float32)
    y = xf * lax.rsqrt(jnp.mean(xf * xf, axis=-1, keepdims=True) + RMS_EPS)
    return (y * g.astype(jnp.float32)).astype(x.dtype)


def layer_norm(x, g, b):
    xf = x.astype(jnp.float32)
    mu = jnp.mean(xf, axis=-1, keepdims=True)
    xc = xf - mu
    y = xc * lax.rsqrt(jnp.mean(xc * xc, axis=-1, keepdims=True) + LN_EPS)
    return (y * g.astype(jnp.float32) + b.astype(jnp.float32)).astype(x.dtype)


def modulate(h, shift, scale):
    return h * (1 + scale[:, None, :]) + shift[:, None, :]


def causal_dwconv(x, w, b=None):
    K, C = w.shape
    xp = jnp.pad(x, ((0, 0), (K - 1, 0), (0, 0)))
    y = lax.conv_general_dilated(xp, w[:, None, :], window_strides=(1,), padding='VALID',
                                 dimension_numbers=('NWC', 'WIO', 'NWC'),
                                 feature_group_count=C)
    if b is not None:
        y = y + b
    return y


def forgetting_attention(q, k, v, log_f):
    B, S, H, Dh = q.shape
    nb = S // Q_BLOCK
    F = jnp.cumsum(log_f, axis=1)
    Fk = jnp.transpose(F, (0, 2, 1))[:, :, None, :]
    qb = jnp.transpose(q.reshape(B, nb, Q_BLOCK, H, Dh), (1, 0, 2, 3, 4))
    Fqb = jnp.transpose(F.reshape(B, nb, Q_BLOCK, H), (1, 0, 3, 2))
    qpos = jnp.arange(S).reshape(nb, Q_BLOCK)
    kpos = jnp.arange(S)
    scale = Dh ** -0.5

    def one_block(args):
        q_blk, Fq_blk, qp = args
        s = jnp.einsum('bqhd,bkhd->bhqk', q_blk, k,
                       preferred_element_type=jnp.float32) * scale
        s = s + Fq_blk[..., None] - Fk
        mask = kpos[None, :] <= qp[:, None]
        s = jnp.where(mask, s, -jnp.inf)
        p = jax.nn.softmax(s, axis=-1)
        return jnp.einsum('bhqk,bkhd->bqhd', p.astype(v.dtype), v)

    out = lax.map(one_block, (qb, Fqb, qpos))
    return jnp.transpose(out, (1, 0, 2, 3, 4)).reshape(B, S, H * Dh)


def setup_inputs(seed: int = 0) -> dict:
    key = jax.random.key(seed)
    ks = jax.random.split(key, 24)
    L, D = DEPTH, D_MODEL
    nrm = jax.random.normal
    x = nrm(ks[0], (BATCH, SEQ, D), jnp.float32)
    c = nrm(ks[1], (BATCH, D), jnp.float32)
    ada_w = nrm(ks[2], (L, D, N_ADA * D), jnp.float32) * (0.5 * D ** -0.5)
    ada_b = nrm(ks[3], (L, N_ADA * D), jnp.float32) * 0.02
    mix_norm_g = 1.0 + 0.05 * nrm(ks[4], (L, D), jnp.float32)
    w_in = nrm(ks[5], (L, D, IN_COLS), jnp.float32) * D ** -0.5
    b_forget = jax.random.uniform(ks[6], (L, ATTN_HEADS), jnp.float32, 1.0, 4.0)
    conf_dw_w = nrm(ks[7], (L, CONF_WIDTH, D_CONF), jnp.float32) * CONF_WIDTH ** -0.5
    conf_dw_b = nrm(ks[8], (L, D_CONF), jnp.float32) * 0.02
    conf_ln_g = 1.0 + 0.05 * nrm(ks[9], (L, D_CONF), jnp.float32)
    conf_ln_b = nrm(ks[10], (L, D_CONF), jnp.float32) * 0.02
    sc_dw_w = nrm(ks[11], (L, SCONV_WIDTH, D_SCONV), jnp.float32) * SCONV_WIDTH ** -0.5
    w_out = nrm(ks[12], (L, D_MIX, D), jnp.float32) * D_MIX ** -0.5
    ffn_norm_g = 1.0 + 0.05 * nrm(ks[13], (L, D), jnp.float32)
    w_up = nrm(ks[14], (L, D, 2 * D_FF), jnp.float32) * D ** -0.5
    ffn_dw_w = nrm(ks[15], (L, FFN_WIDTH, 2 * D_FF), jnp.float32) * FFN_WIDTH ** -0.5
    ffn_dw_b = nrm(ks[16], (L, 2 * D_FF), jnp.float32) * 0.02
    w_down = nrm(ks[17], (L, D_FF, D), jnp.float32) * D_FF ** -0.5
    final_norm_g = 1.0 + 0.05 * nrm(ks[18], (D,), jnp.float32)
    return {"x": x, "c": c, "ada_w": ada_w, "ada_b": ada_b, "mix_norm_g": mix_norm_g,
            "w_in": w_in, "b_forget": b_forget, "conf_dw_w": conf_dw_w, "conf_dw_b": conf_dw_b,
            "conf_ln_g": conf_ln_g, "conf_ln_b": conf_ln_b, "sc_dw_w": sc_dw_w, "w_out": w_out,
            "ffn_norm_g": ffn_norm_g, "w_up": w_up, "ffn_dw_w": ffn_dw_w, "ffn_dw_b": ffn_dw_b,
            "w_down": w_down, "final_norm_g": final_norm_g}


def reference(x, c, ada_w, ada_b, mix_norm_g, w_in, b_forget, conf_dw_w, conf_dw_b,
              conf_ln_g, conf_ln_b, sc_dw_w, w_out, ffn_norm_g, w_up, ffn_dw_w, ffn_dw_b,
              w_down, final_norm_g):
    B, S, _ = x.shape
    split_at = list(np.cumsum([D_ATTN, D_ATTN, D_ATTN, ATTN_HEADS,
                               D_CONF, D_CONF, D_SCONV, D_SCONV]))
    c_act = jax.nn.silu(c)
    for l in range(DEPTH):
        ada = c_act @ ada_w[l] + ada_b[l]
        sh_m, sc_m, g_m, sh_f, sc_f, g_f = jnp.split(ada, N_ADA, axis=-1)

        h = modulate(rms_norm(x, mix_norm_g[l]), sh_m, sc_m)
        proj = h @ w_in[l]
        q, k, v, f_logit, cv, cg, s_x, s_b, s_c = jnp.split(proj, split_at, axis=-1)

        log_f = jax.nn.log_sigmoid((f_logit + b_forget[l]).astype(jnp.float32))
        attn = forgetting_attention(q.reshape(B, S, ATTN_HEADS, ATTN_HEAD_DIM),
                                    k.reshape(B, S, ATTN_HEADS, ATTN_HEAD_DIM),
                                    v.reshape(B, S, ATTN_HEADS, ATTN_HEAD_DIM), log_f)

        conf = cv * jax.nn.sigmoid(cg)
        conf = causal_dwconv(conf, conf_dw_w[l], conf_dw_b[l])
        conf = jax.nn.silu(layer_norm(conf, conf_ln_g[l], conf_ln_b[l]))

        sconv = s_b * causal_dwconv(s_c * s_x, sc_dw_w[l])

        mixed = jnp.concatenate([attn, conf, sconv], axis=-1) @ w_out[l]
        x = x + g_m[:, None, :] * mixed

        h = modulate(rms_norm(x, ffn_norm_g[l]), sh_f, sc_f)
        u = causal_dwconv(h @ w_up[l], ffn_dw_w[l], ffn_dw_b[l])
        gate, val = jnp.split(u, 2, axis=-1)
        x = x + g_f[:, None, :] * ((jax.nn.silu(gate) * val) @ w_down[l])

    return rms_norm(x, final_norm_g)
```

```python
import numpy as np
import concourse.bass as bass
import concourse.mybir as mybir
from concourse.bass_utils import run_bass_kernel_spmd

F32 = mybir.dt.float32
BF16 = mybir.dt.bfloat16
ALU = mybir.AluOpType
AF = mybir.ActivationFunctionType

ENG_NAMES = ("pe", "act", "dve", "pool", "sp")


class Cfg:
    def __init__(self, D=2048, F=5632, L=4, TOK=8192, TW=512, ncores=4):
        self.D, self.F, self.L, self.TOK, self.TW, self.ncores = D, F, L, TOK, TW, ncores
        self.KC = D // 128
        self.DA = D // 2
        self.NH = self.DA // 128
        self.DC = D // 4
        self.NCC = self.DC // 128
        self.DS = D - self.DA - self.DC
        self.NSC = self.DS // 128
        self.NFC = F // 128
        self.NT = TOK // TW
        self.NB = TOK // 128
        self.BPT = TW // 128
        self.IN_COLS = 3 * self.DA + self.NH + 2 * self.DC + 3 * self.DS
        self.q0, self.k0, self.v0 = 0, self.DA, 2 * self.DA
        self.f0 = 3 * self.DA
        self.cv0 = self.f0 + self.NH
        self.cg0 = self.cv0 + self.DC
        self.sx0 = self.cg0 + self.DC
        self.sb0 = self.sx0 + self.DS
        self.sc0 = self.sb0 + self.DS
        self.CW, self.SW, self.FW = 31, 3, 3


class Res:
    __slots__ = ("name", "writer", "readers")

    def __init__(self, name):
        self.name = name
        self.writer = None
        self.readers = []


class Op:
    __slots__ = ("eng", "fn", "deps", "signal", "sigidx", "chan", "count", "is_dma", "small")

    def __init__(self, eng, fn, is_dma=False, chan=None):
        self.eng, self.fn, self.is_dma, self.chan = eng, fn, is_dma, chan
        self.deps = []
        self.signal = False
        self.sigidx = 0
        self.count = 0


class Prog:
    def __init__(self):
        self.ops = []
        self.chan_count = {}
        self.res = {}
        self.bulk = set()

    def R(self, name):
        r = self.res.get(name)
        if r is None:
            r = self.res[name] = Res(name)
        return r

    def _rl(self, xs):
        out = []
        for x in xs:
            out.append(self.R(x) if isinstance(x, str) else x)
        return out

    def add(self, eng, fn, reads=(), writes=(), dma_chan=None, small=False):
        op = Op(eng, fn, is_dma=dma_chan is not None, chan=dma_chan)
        op.small = small
        deps = set()
        for r in self._rl(reads):
            if r.writer is not None:
                deps.add(r.writer)
            r.readers.append(op)
        for w in self._rl(writes):
            if w.writer is not None and not (dma_chan is not None and w.writer.is_dma and w.writer.chan == dma_chan):
                deps.add(w.writer)
            for rd in w.readers:
                if rd is not op:
                    deps.add(rd)
            w.writer = op
            w.readers = []
        for d in deps:
            if d is op:
                continue
            if (not d.is_dma) and d.eng == eng and not op.is_dma and not d.small:
                continue
            op.deps.append(d)
            d.signal = True
        if dma_chan is not None:
            c = self.chan_count.get(dma_chan, 0) + 16
            self.chan_count[dma_chan] = c
            op.count = c
        self.ops.append(op)
        return op

    def emit(self, nc, sems, chan_sems, engines):
        cnt = {e: 0 for e in ENG_NAMES}
        for op in self.ops:
            if not op.is_dma and op.signal:
                cnt[op.eng] += 1
                op.sigidx = cnt[op.eng]
        per_eng = {e: [] for e in ENG_NAMES}
        for op in self.ops:
            per_eng[op.eng].append(op)

        def run_engine(ename, eobj):
            waited = {}
            for op in per_eng[ename]:
                need = {}
                for d in op.deps:
                    if d.is_dma:
                        key, val = ("c", d.chan), (self.chan_count[d.chan] if d.chan in self.bulk else d.count)
                    else:
                        key, val = ("e", d.eng), d.sigidx
                    if val > need.get(key, 0):
                        need[key] = val
                for key, val in need.items():
                    if waited.get(key, 0) >= val:
                        continue
                    sem = chan_sems[key[1]] if key[0] == "c" else sems[key[1]]
                    eobj.wait_ge(sem, val)
                    waited[key] = val
                ins = op.fn(eobj)
                if op.is_dma:
                    ins.then_inc(chan_sems[op.chan], 16)
                elif op.signal:
                    ins.then_inc(sems[op.eng], 1)

        return run_engine, per_eng


def build_program(cfg, debug_outs=False):
    c = cfg
    D, F, L, TOK, TW, KC, NH, NCC, NSC, NFC, NT, NB, BPT = (
        c.D, c.F, c.L, c.TOK, c.TW, c.KC, c.NH, c.NCC, c.NSC, c.NFC, c.NT, c.NB, c.BPT)
    DA, DC, DS = c.DA, c.DC, c.DS
    nc = bass.Bass("TRN2", target_bir_lowering=False)
    P = Prog()

    def din(name, shape, dt=F32):
        return nc.dram_tensor(name, list(shape), dt, kind="ExternalInput").ap()

    def dscr(name, shape, dt):
        kind = "ExternalOutput" if (debug_outs and not name.startswith("wb_")) else "Internal"
        return nc.dram_tensor(name, list(shape), dt, kind=kind).ap()

    xT = din("xT", [D, TOK])
    cT = din("cT", [128, KC])
    ada_w = din("ada_w", [L, D, 6 * D])
    ada_b = din("ada_b", [128, L * 6 * KC])
    g_mix = din("g_mix", [128, L * KC])
    g_ffn = din("g_ffn", [128, L * KC])
    g_fin = din("g_fin", [128, KC])
    w_in = din("w_in", [L, D, c.IN_COLS])
    b_fg = din("b_fg", [NH, L])
    cdw_w = din("cdw_w", [128, L * NCC * c.CW])
    cdw_b = din("cdw_b", [128, L * NCC])
    cln_g = din("cln_g", [128, L * NCC])
    cln_b = din("cln_b", [128, L * NCC])
    sdw_w = din("sdw_w", [128, L * NSC * c.SW])
    w_out = din("w_out", [L, D, D])
    w_up = din("w_up", [L, D, 2 * F])
    fdw_w = din("fdw_w", [128, L * 2 * NFC * c.FW])
    fdw_b = din("fdw_b", [128, L * 2 * NFC])
    w_down = din("w_down", [L, F, D])
    ident_in = din("ident", [128, 128])
    tri_in = din("tri", [128, 128])
    outT = nc.dram_tensor("outT", [D, TOK], F32, kind="ExternalOutput").ap()

    xs = dscr("xs", [D, TOK], F32)
    qT_s = dscr("qT_s", [DA, TOK], BF16)
    kT_s = dscr("kT_s", [DA, TOK], BF16)
    v_s = dscr("v_s", [TOK, DA], BF16)
    Fb_s = dscr("Fb_s", [NH, NB], F32)
    F_s = dscr("F_s", [NH, TOK], F32)
    cat_s = dscr("cat_s", [D, TOK], BF16)

    GC = 512

    def groups_of(n):
        return [(c0, min(GC, n - c0)) for c0 in range(0, n, GC)]

    win_groups = []
    for nm, c0, n in (("q", c.q0, DA), ("k", c.k0, DA), ("v", c.v0, DA), ("f", c.f0, NH),
                      ("cg", c.cg0, DC), ("cv", c.cv0, DC), ("sx", c.sx0, DS),
                      ("sc", c.sc0, DS), ("sb", c.sb0, DS)):
        for (g0, gn) in groups_of(n):
            win_groups.append((nm, c0 + g0, gn, g0))
    wout_groups = [("o", g0, gn, g0) for (g0, gn) in groups_of(D)]
    wup_groups = []
    for (g0, gn) in groups_of(F):
        wup_groups.append(("ug", g0, gn, g0))
        wup_groups.append(("uv", F + g0, gn, g0))
    wdn_groups = [("d", oc * 128, 128, oc * 128) for oc in range(KC)]
    ada_groups = [("a", g0, gn, g0) for (g0, gn) in groups_of(6 * D)]

    SLOT_ELEMS = max(KC * GC, NFC * 128)
    wb = {}
    for l in range(L):
        wb["in", l] = dscr(f"wb_in{l}", [len(win_groups), 128, KC * GC], BF16)
        wb["out", l] = dscr(f"wb_out{l}", [len(wout_groups), 128, KC * GC], BF16)
        wb["up", l] = dscr(f"wb_up{l}", [len(wup_groups), 128, KC * GC], BF16)
        wb["dn", l] = dscr(f"wb_dn{l}", [len(wdn_groups), 128, NFC * 128], BF16)
        wb["ada", l] = dscr(f"wb_ada{l}", [len(ada_groups), 128, KC * GC], BF16)
    wsrc = {"in": w_in, "out": w_out, "up": w_up, "dn": w_down, "ada": ada_w}
    wgroups = {"in": win_groups, "out": wout_groups, "up": wup_groups, "dn": wdn_groups,
               "ada": ada_groups}
    wkc = {"in": KC, "out": KC, "up": KC, "dn": NFC, "ada": KC}

    import contextlib
    es = contextlib.ExitStack()

    def sb(name, shape, dt=F32):
        return es.enter_context(nc.sbuf_tensor(name, list(shape), dt))

    NSLOT = 3
    ring = [sb(f"ring{i}", [128, SLOT_ELEMS], BF16) for i in range(NSLOT)]
    xt = sb("xt", [128, KC, TW])
    hTb = [sb(f"hT{i}", [128, KC, TW], BF16) for i in range(2)]
    hT = hTb[0]
    sq = [sb(f"sq{i}", [128, TW], BF16) for i in range(2)]
    tmpf = [sb(f"tmpf{i}", [128, TW]) for i in range(3)]
    rstd_b = sb("rstd_b", [128, TW])
    MCS = max(NCC, NSC)
    PA_SIZES = [MCS * TW, MCS * TW, NCC * (c.CW - 1 + TW), NSC * (c.SW - 1 + TW), TW, TW]
    PA_F32 = sum(PA_SIZES)
    BIGN = max(NFC * TW, 2 * TOK, 2 * PA_F32)
    BIGN = (BIGN + 511) // 512 * 512
    big = sb("big", [128, BIGN], BF16)
    bigf = big[:, 0:2 * PA_F32].bitcast(F32)
    _o = [0]

    def carve(n):
        v = bigf[:, _o[0]:_o[0] + n]
        _o[0] += n
        return v
    bufA = carve(PA_SIZES[0]).rearrange("p (c t) -> p c t", t=TW)
    accb = carve(PA_SIZES[1]).rearrange("p (c t) -> p c t", t=TW)
    glu = carve(PA_SIZES[2]).rearrange("p (c t) -> p c t", t=c.CW - 1 + TW)
    tbuf = carve(PA_SIZES[3]).rearrange("p (c t) -> p c t", t=c.SW - 1 + TW)
    lnm = carve(TW)
    lnv = carve(TW)
    PA_NAMES = ([f"bufA{i}" for i in range(MCS)] + [f"accb{i}" for i in range(MCS)] + [f"glu{i}" for i in range(NCC)]
                + [f"tbuf{i}" for i in range(NSC)] + ["glu_h", "tbuf_h", "lnm", "lnv"])
    stage = [sb(f"stage{i}", [128, TW], BF16) for i in range(4)]
    bar_t = sb("bar_t", [128, 2])
    qbuf = [sb(f"qbuf{i}", [128, TW], BF16) for i in range(2)]
    bias4 = [sb(f"bias4_{i}", [128, BPT, NB]) for i in range(2)]
    fsp = tmpf[0][0:NH, :]
    pbuf = [sb(f"pbuf{i}", [128, TW], BF16) for i in range(4)]
    ft = sb("ft", [NH, TW])
    fprev = sb("fprev", [NH, 1])
    fb4 = sb("fb4", [NH, BPT])
    fones = sb("fones", [NH, TW])
    fcol = sb("fcol", [128, NB, NH])
    ft0b = sb("ft0b", [128, NH * NB])
    fhalo2 = [sb(f"fhalo2_{i}", [128, 2 * NFC, 2]) for i in range(2)]
    ag = [sb(f"ag{i}", [128, TW]) for i in range(1)]
    av = [sb(f"av{i}", [128, TW]) for i in range(1)]
    c_sb = sb("c_sb", [128, KC])
    cact = sb("cact", [128, KC], BF16)
    adab_sb = sb("adab_sb", [128, L * 6 * KC])
    ada_sb = sb("ada_sb", [128, L * 6 * KC])
    gmix_sb = sb("gmix_sb", [128, L * KC])
    gffn_sb = sb("gffn_sb", [128, L * KC])
    gfin_sb = sb("gfin_sb", [128, KC])
    gs_m = sb("gs_m", [128, L * KC])
    gs_f = sb("gs_f", [128, L * KC])
    bfg_sb = sb("bfg_sb", [NH, L])
    negb = sb("negb", [NH, L])
    cdww = sb("cdww", [128, L * NCC * c.CW])
    cdwb = sb("cdwb", [128, L * NCC])
    clng = sb("clng", [128, L * NCC])
    clnb = sb("clnb", [128, L * NCC])
    sdww = sb("sdww", [128, L * NSC * c.SW])
    fdww = sb("fdww", [128, L * 2 * NFC * c.FW])
    fdwb = sb("fdwb", [128, L * 2 * NFC])
    ident = sb("ident_sb", [128, 128])
    tri_f = sb("tri_f", [128, 128])
    tri = sb("tri_b", [128, 128], BF16)
    ident_bf = sb("ident_bf", [128, 128], BF16)
    ones_bf = sb("ones_bf", [128, 128], BF16)
    ones_f = sb("ones_f", [128, 128])
    psum = [es.enter_context(nc.psum_tensor(f"ps{i}", [128, 512], F32)) for i in range(8)]

    cat_off = BIGN - KC * TW
    cat_v = big[:, cat_off:cat_off + KC * TW].rearrange("p (k t) -> p k t", t=TW)
    K_v = big[:, 0:TOK]
    V_v = big[:, TOK:2 * TOK].rearrange("p (b d) -> p b d", d=128)

    def bigres(lo, hi):
        return [f"big{i}" for i in range(lo // 512, (hi + 511) // 512)]

    INV_SQRT_DH = 1.0 / np.sqrt(128.0)

    class WStream:
        def __init__(self):
            self.seq = []
            self.recording = True
            self.pos = 0
            self.loaded = 0

        def _issue(self, i):
            kind, l, gi = self.seq[i]
            slot = i % NSLOT
            _, c0, ncols, _ = wgroups[kind][gi]
            kcw = wkc[kind]
            n = kcw * (ncols if kind != "dn" else 128)
            src = wb[kind, l][gi][:, 0:n]
            dst = ring[slot][:, 0:n]
            P.add("sp", lambda e, dst=dst, src=src: e.dma_start(out=dst, in_=src),
                  reads=[f"wb_{kind}{l}_{gi}"], writes=[f"ring{slot}"], dma_chan=f"ring{slot}")

        def get(self, kind, l, gi):
            if self.recording:
                self.seq.append((kind, l, gi))
                return None
            i = self.pos
            assert self.seq[i] == (kind, l, gi)
            self.pos += 1
            while self.loaded < min(len(self.seq), i + NSLOT - 1):
                self._issue(self.loaded)
                self.loaded += 1
            slot = i % NSLOT
            _, c0, ncols, _ = wgroups[kind][gi]
            kcw = wkc[kind]
            nn = ncols if kind != "dn" else 128
            view = ring[slot][:, 0:kcw * nn].rearrange("p (k n) -> p k n", n=nn)
            return view, f"ring{slot}"

    WS = WStream()
    dbg = {}

    def dbg_dump(name, src_ap, shape, dt, reads):
        if not debug_outs or name in dbg or WS.recording:
            return
        t = nc.dram_tensor(name, list(shape), dt, kind="ExternalOutput").ap()
        dbg[name] = t
        P.add("pool", lambda e: e.dma_start(out=t, in_=src_ap), reads=reads, writes=[name], dma_chan="dbg_" + name)

    state = {"mm": 0}

    def next_bank(nb=6):
        b = state["mm"] % nb
        state["mm"] += 1
        return b

    cast_q = []

    def pump_casts(n):
        for _ in range(min(n, len(cast_q))):
            cast_q.pop(0)()

    def emit_casts(l):
        for kind in ("ada", "in", "out", "up", "dn"):
            P.bulk.add(f"cast_{kind}{l}")
            src_all = wsrc[kind][l]
            kcw = wkc[kind]
            for gi, (_, c0, ncols, _) in enumerate(wgroups[kind]):
                src = src_all[:, c0:c0 + ncols].rearrange("(k p) n -> p k n", p=128)
                dst = wb[kind, l][gi][:, 0:kcw * ncols].rearrange("p (k n) -> p k n", n=ncols)
                cast_q.append(lambda dst=dst, src=src, kind=kind, gi=gi: P.add(
                    "pool", lambda e: e.dma_start(out=dst, in_=src, max_dma_last_dim=8192),
                    reads=[], writes=[f"wb_{kind}{l}_{gi}"], dma_chan=f"cast_{kind}{l}"))

    def act_op(fn, reads, writes):
        return P.add("act", fn, reads, writes)

    def dve_op(fn, reads, writes, small=False):
        return P.add("dve", fn, reads, writes, small=small)

    def pool_op(fn, reads, writes):
        return P.add("pool", fn, reads, writes)

    def pe_op(fn, reads, writes):
        return P.add("pe", fn, reads, writes)

    def proj_fm(slot_v, slot_res, oc, kcw, rhs_tile, rhs_res, n, bank):
        def fn(e):
            ins = None
            for k in range(kcw):
                ins = e.matmul(psum[bank][:, 0:n], slot_v[:, k, oc * 128:(oc + 1) * 128],
                               rhs_tile[:, k, 0:n], start=(k == 0), stop=(k == kcw - 1))
            return ins
        pe_op(fn, reads=[slot_res] + rhs_res, writes=[f"ps{bank}"])

    def load_x_tile(l, t):
        src = (xT if l == 0 else xs)[:, t * TW:(t + 1) * TW].rearrange("(k p) t -> p k t", p=128)
        half = KC // 2
        for hf in range(2):
            P.add("sp", lambda e, hf=hf: e.dma_start(out=xt[:, hf * half:(hf + 1) * half, :],
                                                      in_=src[:, hf * half:(hf + 1) * half, :]),
                  reads=([f"xs_{t}"] if l > 0 else []), writes=[f"xt{k}" for k in range(hf * half, (hf + 1) * half)],
                  dma_chan=f"xt{hf}")

    def rms_to_h(gs_ap, sh_ap, out_is_final=False, gfin=None, hb=0):
        hT = hTb[hb]
        SB = 6
        for k in range(KC):
            s = sq[k % 2]
            act_op(lambda e, k=k, s=s: e.activation(out=s[:], in_=xt[:, k, :], func=AF.Square),
                   reads=[f"xt{k}"], writes=[f"sq{k % 2}"])
            pe_op(lambda e, k=k, s=s: e.matmul(psum[SB][:, 0:TW], ones_bf[:], s[:], start=(k == 0), stop=(k == KC - 1)),
                  reads=[f"sq{k % 2}", "consts"], writes=[f"ps{SB}"])
        act_op(lambda e: e.activation(out=rstd_b[:], in_=psum[SB][:, 0:TW], func=AF.Sqrt, scale=1.0 / D, bias=eps_rms[:, 0:1]),
               reads=[f"ps{SB}", "consts"], writes=["rstd_b"])
        dve_op(lambda e: e.reciprocal(out=rstd_b[:], in_=rstd_b[:]), reads=["rstd_b"], writes=["rstd_b"])
        for k in range(KC):
            if out_is_final:
                dve_op(lambda e, k=k: e.scalar_tensor_tensor(out=xt[:, k, :], in0=xt[:, k, :], scalar=gfin[:, k:k + 1],
                                                             in1=rstd_b[:], op0=ALU.mult, op1=ALU.mult),
                       reads=[f"xt{k}", "rstd_b", "consts"], writes=[f"xt{k}"])
            else:
                tf = tmpf[k % 3]
                dve_op(lambda e, k=k, tf=tf: e.tensor_tensor(out=tf[:], in0=xt[:, k, :], in1=rstd_b[:], op=ALU.mult),
                       reads=[f"xt{k}", "rstd_b"], writes=[f"tmpf{k % 3}"])
                act_op(lambda e, k=k, tf=tf: e.activation(out=hT[:, k, :], in_=tf[:], func=AF.Identity,
                                                          scale=gs_ap[:, k:k + 1], bias=sh_ap[:, k:k + 1]),
                       reads=[f"tmpf{k % 3}", "ada"], writes=[f"hT{hb}_{k}"])

    hT_res = [f"hT0_{k}" for k in range(KC)]
    hT_resb = [[f"hT{b}_{k}" for k in range(KC)] for b in range(2)]

    eps_rms = sb("eps_rms", [128, 1])
    eps_ln = sb("eps_ln", [128, 1])
    one_c = sb("one_c", [128, 1])

    def prep():
        loads = [(c_sb, cT), (adab_sb, ada_b), (gmix_sb, g_mix), (gffn_sb, g_ffn), (gfin_sb, g_fin),
                 (bfg_sb, b_fg), (cdww, cdw_w), (cdwb, cdw_b), (clng, cln_g), (clnb, cln_b),
                 (sdww, sdw_w), (fdww, fdw_w), (fdwb, fdw_b), (ident, ident_in), (tri_f, tri_in)]
        for i, (dst, src) in enumerate(loads):
            P.add("sp", lambda e, dst=dst, src=src: e.dma_start(out=dst[:], in_=src),
                  reads=[], writes=["consts_raw"], dma_chan="consts")
        dve_op(lambda e: e.memset(ones_f[:], 1.0), [], ["consts0"], small=True)
        dve_op(lambda e: e.memset(eps_rms[:], 1e-6), [], ["consts0"], small=True)
        dve_op(lambda e: e.memset(eps_ln[:], 1e-5), [], ["consts0"], small=True)
        dve_op(lambda e: e.memset(one_c[:], 1.0), [], ["consts0"], small=True)
        dve_op(lambda e: e.memset(fones[:], 1.0), [], ["consts0"], small=True)
        dve_op(lambda e: e.tensor_copy(out=ones_bf[:], in_=ones_f[:]), ["consts0"], ["consts"], small=True)
        dve_op(lambda e: e.tensor_scalar(out=tri[:], in0=tri_f[:], scalar1=-1.0, scalar2=30000.0, op0=ALU.add, op1=ALU.mult),
               ["consts_raw"], ["consts"], small=True)
        dve_op(lambda e: e.tensor_copy(out=ident_bf[:], in_=ident[:]), ["consts_raw"], ["consts"], small=True)
        dve_op(lambda e: e.tensor_scalar(out=negb[:], in0=bfg_sb[:], scalar1=-1.0, scalar2=None, op0=ALU.mult),
               ["consts_raw"], ["consts"], small=True)
        act_op(lambda e: e.activation(out=cact[:], in_=c_sb[:], func=AF.Silu), ["consts_raw"], ["consts"])

    def ada_layer(l):
        AB = 7
        n6 = 6 * KC
        for gi, (_, c0, ncols, _) in enumerate(ada_groups):
            got = WS.get("ada", l, gi)
            if got is None:
                continue
            slot_v, slot_res = got
            for oc in range(ncols // 128):
                col = c0 // 128 + oc

                def fn(e, oc=oc, col=col, slot_v=slot_v):
                    ins = None
                    for k in range(KC):
                        ins = e.matmul(psum[AB][:, col:col + 1], slot_v[:, k, oc * 128:(oc + 1) * 128],
                                       cact[:, k:k + 1], start=(k == 0), stop=(k == KC - 1))
                    return ins
                pe_op(fn, reads=[slot_res, "consts"], writes=[f"ps{AB}"])
        if WS.recording:
            return
        dve_op(lambda e: e.tensor_tensor(out=ada_sb[:, l * n6:(l + 1) * n6], in0=psum[AB][:, 0:n6],
                                         in1=adab_sb[:, l * n6:(l + 1) * n6], op=ALU.add),
               [f"ps{AB}", "consts_raw"], ["ada"], small=True)
        a0 = l * n6
        dve_op(lambda e: e.scalar_tensor_tensor(out=gs_m[:, l * KC:(l + 1) * KC], in0=ada_sb[:, a0 + KC:a0 + 2 * KC],
                                                scalar=1.0, in1=gmix_sb[:, l * KC:(l + 1) * KC], op0=ALU.add, op1=ALU.mult),
               ["ada", "consts_raw"], ["ada"], small=True)
        dve_op(lambda e: e.scalar_tensor_tensor(out=gs_f[:, l * KC:(l + 1) * KC], in0=ada_sb[:, a0 + 4 * KC:a0 + 5 * KC],
                                                scalar=1.0, in1=gffn_sb[:, l * KC:(l + 1) * KC], op0=ALU.add, op1=ALU.mult),
               ["ada", "consts_raw"], ["ada"], small=True)

    def ada_vec(l, which):
        dbg_dump("dbg_ada", ada_sb[:], [128, L * 6 * KC], F32, ["ada"])
        dbg_dump("dbg_cact", cact[:], [128, KC], BF16, ["consts"])
        a0 = l * 6 * KC + which * KC
        return ada_sb[:, a0:a0 + KC]

    def barrier_big():
        dve_op(lambda e: e.memset(bar_t[:], 0.0), [], PA_NAMES + bigres(0, BIGN), small=True)

    def phaseA(l):
        rec = WS.recording
        if not rec:
            barrier_big()
            for ci in range(NCC):
                dve_op(lambda e, ci=ci: e.memset(glu[:, ci, 0:c.CW - 1], 0.0), [], ["glu_h"], small=True)
            for ci in range(NSC):
                dve_op(lambda e, ci=ci: e.memset(tbuf[:, ci, 0:c.SW - 1], 0.0), [], ["tbuf_h"], small=True)
            dve_op(lambda e: e.memset(fprev[:], 0.0), [], ["fprev"], small=True)
            load_x_tile(l, 0)
        sstate = {"st": 0, "vs": 0}
        NORM_AT = min(3, len(win_groups) - 1)
        if not rec:
            rms_to_h(gs_m[:, l * KC:(l + 1) * KC], ada_vec(l, 0), hb=0)
            if NT > 1:
                load_x_tile(l, 1)
        for t in range(NT):
            t0 = t * TW
            hT = hTb[t % 2]
            hT_res = hT_resb[t % 2]
            for gi, (kind, c0, ncols, g0) in enumerate(win_groups):
                got = WS.get("in", l, gi)
                if rec:
                    continue
                if gi == NORM_AT and t + 1 < NT:
                    rms_to_h(gs_m[:, l * KC:(l + 1) * KC], ada_vec(l, 0), hb=(t + 1) % 2)
                    if t + 2 < NT:
                        load_x_tile(l, t + 2)
                slot_v, slot_res = got
                if kind == "v":
                    for tb in range(BPT):
                        bank = next_bank()

                        def fn(e, tb=tb, bank=bank, slot_v=slot_v, ncols=ncols, hT=hT):
                            ins = None
                            for k in range(KC):
                                ins = e.matmul(psum[bank][:, 0:ncols], hT[:, k, tb * 128:(tb + 1) * 128],
                                               slot_v[:, k, 0:ncols], start=(k == 0), stop=(k == KC - 1))
                            return ins
                        pe_op(fn, reads=[slot_res] + hT_res, writes=[f"ps{bank}"])
                        vi = sstate["st"] % 4
                        sstate["st"] += 1
                        vsb = stage[vi]
                        act_op(lambda e, bank=bank, vsb=vsb, ncols=ncols: e.activation(out=vsb[:, 0:ncols], in_=psum[bank][:, 0:ncols], func=AF.Copy),
                               [f"ps{bank}"], [f"stage{vi}"])
                        dst = v_s[t0 + tb * 128:t0 + (tb + 1) * 128, g0:g0 + ncols]
                        P.add("pool", lambda e, dst=dst, vsb=vsb, ncols=ncols: e.dma_start(out=dst, in_=vsb[:, 0:ncols]),
                              reads=[f"stage{vi}"], writes=[f"v_s_{t}_{tb}_{g0 // GC}"], dma_chan=f"stage{vi}")
                    continue
                if kind == "f":
                    bank = next_bank()

                    def fn(e, bank=bank, slot_v=slot_v, hT=hT):
                        ins = None
                        for k in range(KC):
                            ins = e.matmul(psum[bank][0:NH, 0:TW], slot_v[:, k, 0:NH], hT[:, k, :],
                                           start=(k == 0), stop=(k == KC - 1))
                        return ins
                    pe_op(fn, reads=[slot_res] + hT_res, writes=[f"ps{bank}"])
                    act_op(lambda e, bank=bank: e.activation(out=fsp[:], in_=psum[bank][0:NH, 0:TW], func=AF.Exp,
                                                             scale=-1.0, bias=negb[:, l:l + 1]),
                           [f"ps{bank}", "consts"], ["tmpf0"])
                    act_op(lambda e: e.activation(out=fsp[:], in_=fsp[:], func=AF.Ln, scale=1.0, bias=one_c[0:NH, 0:1]),
                           ["tmpf0", "consts0"], ["tmpf0"])
                    dve_op(lambda e: e.tensor_tensor_scan(out=ft[:], data0=fones[:], data1=fsp[:], initial=fprev[:, 0:1],
                                                          op0=ALU.mult, op1=ALU.subtract),
                           ["tmpf0", "fprev", "consts0"], ["ft"], small=True)
                    dve_op(lambda e: e.tensor_copy(out=fprev[:], in_=ft[:, TW - 1:TW]), ["ft"], ["fprev"], small=True)
                    P.add("pool", lambda e, t0=t0: e.dma_start(out=F_s[:, t0:t0 + TW], in_=ft[:]),
                          reads=["ft"], writes=[f"F_s_{t}"], dma_chan="frow_st")
                    src = ft[:, 0:TW].rearrange("h (b p) -> h b p", p=128)[:, :, 0]
                    dve_op(lambda e, src=src: e.tensor_copy(out=fb4[:], in_=src), ["ft"], ["fb4"], small=True)
                    P.add("pool", lambda e, t=t: e.dma_start(out=Fb_s[:, t * BPT:(t + 1) * BPT], in_=fb4[:]),
                          reads=["fb4"], writes=[f"Fb_s_{t}"], dma_chan="fb")
                    for bb in range(BPT):
                        tbank = next_bank()
                        pe_op(lambda e, bb=bb, tbank=tbank: e.transpose(out=psum[tbank][:, 0:NH], in_=ft[0:NH, bb * 128:(bb + 1) * 128],
                                                                         identity=ident[0:NH, 0:NH]),
                              ["ft", "consts_raw"], [f"ps{tbank}"])
                        kb = t * BPT + bb
                        dve_op(lambda e, tbank=tbank, kb=kb: e.tensor_copy(out=fcol[:, kb, :], in_=psum[tbank][:, 0:NH]),
                               [f"ps{tbank}"], ["fcol"])
                    continue
                for oc in range(ncols // 128):
                    ci = (g0 // 128) + oc
                    bank = next_bank()
                    proj_fm(slot_v, slot_res, oc, KC, hT, hT_res, TW, bank)
                    pr = f"ps{bank}"
                    if kind in ("q", "k"):
                        si = sstate["st"] % 4
                        sstate["st"] += 1
                        stg = stage[si]
                        if kind == "q":
                            act_op(lambda e, bank=bank, stg=stg: e.activation(out=stg[:], in_=psum[bank][:, 0:TW], func=AF.Copy),
                                   [pr], [f"stage{si}"])
                        else:
                            dve_op(lambda e, bank=bank, stg=stg: e.tensor_copy(out=stg[:], in_=psum[bank][:, 0:TW]),
                                   [pr], [f"stage{si}"])
                        dstT = (qT_s if kind == "q" else kT_s)[ci * 128:(ci + 1) * 128, t0:t0 + TW]
                        P.add("pool", lambda e, dstT=dstT, stg=stg: e.dma_start(out=dstT, in_=stg[:]),
                              reads=[f"stage{si}"], writes=[("qT_s" if kind == "q" else "kT_s") + f"_{ci}_{t}"], dma_chan=f"stage{si}")
                    elif kind == "cg":
                        act_op(lambda e, bank=bank, ci=ci: e.activation(out=bufA[:, ci, :], in_=psum[bank][:, 0:TW], func=AF.Sigmoid),
                               [pr], [f"bufA{ci}"])
                    elif kind == "cv":
                        H = c.CW - 1
                        dve_op(lambda e, bank=bank, ci=ci, H=H: e.tensor_tensor(out=glu[:, ci, H:H + TW], in0=psum[bank][:, 0:TW],
                                                                           in1=bufA[:, ci, :], op=ALU.mult),
                               [pr, f"bufA{ci}"], [f"glu{ci}"])
                        wbase = (l * NCC + ci) * c.CW

                        def convfn(e, ci=ci, wbase=wbase, H=H):
                            ins = e.tensor_scalar(out=accb[:, ci, :], in0=glu[:, ci, H:H + TW],
                                                  scalar1=cdww[:, wbase + H:wbase + H + 1],
                                                  scalar2=cdwb[:, l * NCC + ci:l * NCC + ci + 1], op0=ALU.mult, op1=ALU.add)
                            for kk in range(H):
                                ins = e.scalar_tensor_tensor(out=accb[:, ci, :], in0=glu[:, ci, kk:kk + TW],
                                                             scalar=cdww[:, wbase + kk:wbase + kk + 1], in1=accb[:, ci, :],
                                                             op0=ALU.mult, op1=ALU.add)
                            return ins
                        dve_op(convfn, [f"glu{ci}", "glu_h", "consts_raw"], [f"accb{ci}"])
                        act_op(lambda e, ci=ci, H=H: e.activation(out=glu[:, ci, 0:H], in_=glu[:, ci, TW:TW + H], func=AF.Copy),
                               [f"glu{ci}", f"accb{ci}"], ["glu_h"])
                        if ci == NCC - 1:
                            conf_ln_out(l, t0)
                    elif kind == "sx":
                        act_op(lambda e, bank=bank, ci=ci: e.activation(out=bufA[:, ci, :], in_=psum[bank][:, 0:TW], func=AF.Copy),
                               [pr], [f"bufA{ci}"])
                    elif kind == "sc":
                        H = c.SW - 1
                        dve_op(lambda e, bank=bank, ci=ci, H=H: e.tensor_tensor(out=tbuf[:, ci, H:H + TW], in0=psum[bank][:, 0:TW],
                                                                                 in1=bufA[:, ci, :], op=ALU.mult),
                               [pr, f"bufA{ci}"], [f"tbuf{ci}"])
                        wbase = (l * NSC + ci) * c.SW

                        def sconvfn(e, ci=ci, wbase=wbase, H=H):
                            ins = e.tensor_scalar(out=accb[:, ci, :], in0=tbuf[:, ci, H:H + TW],
                                                  scalar1=sdww[:, wbase + H:wbase + H + 1], scalar2=None, op0=ALU.mult)
                            for kk in range(H):
                                ins = e.scalar_tensor_tensor(out=accb[:, ci, :], in0=tbuf[:, ci, kk:kk + TW],
                                                             scalar=sdww[:, wbase + kk:wbase + kk + 1], in1=accb[:, ci, :],
                                                             op0=ALU.mult, op1=ALU.add)
                            return ins
                        dve_op(sconvfn, [f"tbuf{ci}", "tbuf_h", "consts_raw"], [f"accb{ci}"])
                        act_op(lambda e, ci=ci, H=H: e.activation(out=tbuf[:, ci, 0:H], in_=tbuf[:, ci, TW:TW + H], func=AF.Copy),
                               [f"tbuf{ci}", f"accb{ci}"], ["tbuf_h"])
                    elif kind == "sb":
                        si = sstate["st"] % 4
                        sstate["st"] += 1
                        stg = stage[si]
                        dve_op(lambda e, bank=bank, ci=ci, stg=stg: e.tensor_tensor(out=stg[:], in0=psum[bank][:, 0:TW],
                                                                                    in1=accb[:, ci, :], op=ALU.mult),
                               [pr, f"accb{ci}"], [f"stage{si}"])
                        r0 = DA + DC + ci * 128
                        dstT = cat_s[r0:r0 + 128, t0:t0 + TW]
                        P.add("pool", lambda e, dstT=dstT, stg=stg: e.dma_start(out=dstT, in_=stg[:]),
                              reads=[f"stage{si}"], writes=[f"cat_s_{r0 // 128}_{t}"], dma_chan=f"stage{si}")

        def _unused():
            pass

    def conf_ln_out(l, t0):
        MB, VB = 6, 7
        for ci in range(NCC):
            pe_op(lambda e, ci=ci: e.matmul(psum[MB][:, 0:TW], ones_f[:], accb[:, ci, :], start=(ci == 0), stop=(ci == NCC - 1)),
                  [f"accb{ci}", "consts0"], [f"ps{MB}"])
        for ci in range(NCC):
            tf = tmpf[ci % 3]
            act_op(lambda e, ci=ci, tf=tf: e.activation(out=tf[:], in_=accb[:, ci, :], func=AF.Square),
                   [f"accb{ci}"], [f"tmpf{ci % 3}"])
            pe_op(lambda e, ci=ci, tf=tf: e.matmul(psum[VB][:, 0:TW], ones_f[:], tf[:], start=(ci == 0), stop=(ci == NCC - 1)),
                  [f"tmpf{ci % 3}", "consts0"], [f"ps{VB}"])
        dbg_dump("dbg_accb", accb[:], [128, max(NCC, NSC), TW], F32, [f"accb{ci}" for ci in range(NCC)])
        dbg_dump("dbg_glu", glu[:], [128, NCC, c.CW - 1 + TW], F32, [f"glu{ci}" for ci in range(NCC)])
        act_op(lambda e: e.activation(out=lnm[:], in_=psum[MB][:, 0:TW], func=AF.Copy, scale=1.0 / DC),
               [f"ps{MB}"], ["lnm"])
        dbg_dump("dbg_lnm", lnm[:], [128, TW], F32, ["lnm"])
        dve_op(lambda e: e.tensor_tensor(out=lnv[:], in0=lnm[:], in1=lnm[:], op=ALU.mult), ["lnm"], ["lnv"])
        dve_op(lambda e: e.scalar_tensor_tensor(out=lnv[:], in0=psum[VB][:, 0:TW], scalar=1.0 / DC, in1=lnv[:],
                                                op0=ALU.mult, op1=ALU.subtract),
               [f"ps{VB}", "lnv"], ["lnv"])
        act_op(lambda e: e.activation(out=lnv[:], in_=lnv[:], func=AF.Sqrt, scale=1.0, bias=eps_ln[:, 0:1]),
               ["lnv", "consts0"], ["lnv"])
        dve_op(lambda e: e.reciprocal(out=lnv[:], in_=lnv[:]), ["lnv"], ["lnv"])
        dbg_dump("dbg_lnv", lnv[:], [128, TW], F32, ["lnv"])
        for ci in range(NCC):
            dve_op(lambda e, ci=ci: e.tensor_tensor(out=accb[:, ci, :], in0=accb[:, ci, :], in1=lnm[:], op=ALU.subtract),
                   [f"accb{ci}", "lnm"], [f"accb{ci}"])
            dve_op(lambda e, ci=ci: e.tensor_tensor(out=accb[:, ci, :], in0=accb[:, ci, :], in1=lnv[:], op=ALU.mult),
                   [f"accb{ci}", "lnv"], [f"accb{ci}"])
            si = ci % 4
            stg = stage[si]
            j = l * NCC + ci
            act_op(lambda e, ci=ci, stg=stg, j=j: e.activation(out=stg[:], in_=accb[:, ci, :], func=AF.Silu,
                                                              scale=clng[:, j:j + 1], bias=clnb[:, j:j + 1]),
                   [f"accb{ci}", "consts_raw"], [f"stage{si}"])
            r0 = DA + ci * 128
            dstT = cat_s[r0:r0 + 128, t0:t0 + TW]
            P.add("pool", lambda e, dstT=dstT, stg=stg: e.dma_start(out=dstT, in_=stg[:]),
                  reads=[f"stage{si}"], writes=[f"cat_s_{r0 // 128}_{t0 // TW}"], dma_chan=f"stage{si}")

    def phaseB(l):
        if WS.recording:
            return
        barrier_big()
        P.add("sp", lambda e: e.dma_start(out=ft0b[:], in_=Fb_s.rearrange("h b -> (h b)").partition_broadcast(128)),
              reads=[f"Fb_s_{tt}" for tt in range(NT)], writes=["ft0b"], dma_chan="ft0b")
        SB_, OB_, LB_ = (0, 1, 2, 3), (4, 5), (6, 7)
        scount = 0
        for h in range(NH):
            P.add("sp", lambda e, h=h: e.dma_start(out=K_v, in_=kT_s[h * 128:(h + 1) * 128, :]),
                  reads=[f"kT_s_{h}_{tt}" for tt in range(NT)], writes=bigres(0, TOK), dma_chan="Kld")
            vsrc = v_s[:, h * 128:(h + 1) * 128].rearrange("(b p) d -> p b d", p=128)
            VB_ = 16
            for b0 in range(0, NB, VB_):
                b1 = min(NB, b0 + VB_)
                P.add("sp", lambda e, vsrc=vsrc, b0=b0, b1=b1: e.dma_start(out=V_v[:, b0:b1, :], in_=vsrc[:, b0:b1, :]),
                      reads=[f"v_s_{bb // BPT}_{bb % BPT}_{(h * 128) // GC}" for bb in range(b0, b1)],
                      writes=bigres(TOK + b0 * 128, TOK + b1 * 128), dma_chan="Vld")
            for qt in range(NT):
                qb = qbuf[qt % 2]
                qres = f"qbuf{qt % 2}"
                P.add("sp", lambda e, qb=qb, h=h, qt=qt: e.dma_start(out=qb[:], in_=qT_s[h * 128:(h + 1) * 128, qt * TW:(qt + 1) * TW]),
                      reads=[f"qT_s_{h}_{qt}"], writes=[qres], dma_chan=qres)
                b4 = bias4[qt % 2]
                b4res = f"bias4_{qt % 2}"
                nkb = BPT * (qt + 1)
                SQD = float(np.sqrt(128.0))
                cidx = h * NB + qt * BPT
                dve_op(lambda e, b4=b4, h=h, nkb=nkb, cidx=cidx: e.tensor_scalar(
                    out=b4[:, 0, 0:nkb], in0=fcol[:, 0:nkb, h], scalar1=-1.0,
                    scalar2=ft0b[:, cidx:cidx + 1], op0=ALU.mult, op1=ALU.add),
                    ["fcol", "ft0b"], [b4res])
                fr = tmpf[qt % 2]
                frres = f"tmpf{qt % 2}"
                dr = sq[qt % 2]
                drres = f"sq{qt % 2}"
                P.add("sp", lambda e, fr=fr, h=h, qt=qt: e.dma_start(out=fr[0:1, :], in_=F_s[h:h + 1, qt * TW:(qt + 1) * TW]),
                      reads=[f"F_s_{qt}"], writes=[frres], dma_chan="frow" + str(qt % 2))
                dve_op(lambda e, fr=fr, dr=dr: e.tensor_scalar(out=dr[0:1, :], in0=fr[0:1, :], scalar1=fr[0:1, 0:1], scalar2=SQD,
                                                               op0=ALU.subtract, op1=ALU.mult),
                       [frres], [drres], small=True)
                ob, lb = OB_[qt % 2], LB_[qt % 2]
                blk = []
                for kb in range(nkb):
                    blk.append((kb, SB_[scount % 4], pbuf[scount % 4], f"pbuf{scount % 4}"))
                    scount += 1

                def emit_S(kb, sbk, pb, pres, qb=qb, qres=qres, b4=b4, b4res=b4res, qt=qt, dr=dr, drres=drres):
                    jmin = max(0, kb - BPT * qt)
                    c0 = jmin * 128
                    diag = kb >= BPT * qt
                    kres = bigres(kb * 128, (kb + 1) * 128)

                    def sfn(e):
                        e.matmul(psum[sbk][:, c0:TW], K_v[:, kb * 128:(kb + 1) * 128], qb[:, c0:TW], start=True, stop=False)
                        ins = e.matmul(psum[sbk][:, c0:TW], ones_bf[0:1, :], dr[0:1, c0:TW], start=False, stop=(not diag))
                        if diag:
                            ins = e.matmul(psum[sbk][:, c0:c0 + 128], ident_bf[:], tri[:], start=False, stop=True)
                        return ins
                    pe_op(sfn, kres + [qres, drres, "consts"], [f"ps{sbk}"])
                    act_op(lambda e: e.activation(out=pb[:, c0:TW], in_=psum[sbk][:, c0:TW], func=AF.Exp,
                                                  scale=float(INV_SQRT_DH), bias=b4[:, 0, kb:kb + 1]),
                           [f"ps{sbk}", b4res], [pres])

                def emit_PV(kb, sbk, pb, pres, ob=ob, lb=lb, qt=qt, nkb=nkb):
                    jmin = max(0, kb - BPT * qt)
                    c0 = jmin * 128
                    first, last = (kb == 0), (kb == nkb - 1)
                    vres = bigres(TOK + kb * 128, TOK + (kb + 1) * 128)
                    pe_op(lambda e: e.matmul(psum[ob][:, c0:TW], V_v[:, kb, :], pb[:, c0:TW], start=first, stop=last),
                          vres + [pres], [f"ps{ob}"])
                    pe_op(lambda e: e.matmul(psum[lb][:, c0:TW], ones_bf[:], pb[:, c0:TW], start=first, stop=last),
                          [pres, "consts"], [f"ps{lb}"])

                SKEW = 2
                for i in range(nkb + SKEW):
                    if i < nkb:
                        emit_S(*blk[i])
                    if i - SKEW >= 0:
                        emit_PV(*blk[i - SKEW])
                dve_op(lambda e, lb=lb: e.reciprocal(out=rstd_b[:], in_=psum[lb][:, 0:TW]), [f"ps{lb}"], ["rstd_b"])
                si = qt % 4
                stg = stage[si]
                dve_op(lambda e, ob=ob, stg=stg: e.tensor_tensor(out=stg[:], in0=psum[ob][:, 0:TW], in1=rstd_b[:], op=ALU.mult),
                       [f"ps{ob}", "rstd_b"], [f"stage{si}"])
                dstT = cat_s[h * 128:(h + 1) * 128, qt * TW:(qt + 1) * TW]
                P.add("pool", lambda e, dstT=dstT, stg=stg: e.dma_start(out=dstT, in_=stg[:]),
                      reads=[f"stage{si}"], writes=[f"cat_s_{h}_{qt}"], dma_chan=f"stage{si}")
                pump_casts(1)

    def phaseC(l):
        rec = WS.recording
        if not rec:
            pump_casts(10 ** 6)
            dve_op(lambda e: e.memset(fhalo2[0][:], 0.0), [], ["fhalo"], small=True)
            dve_op(lambda e: e.memset(fhalo2[1][:], 0.0), [], ["fhalo"], small=True)
        last_layer = (l == L - 1)
        catres = bigres(cat_off, cat_off + KC * TW)
        for t in range(NT):
            t0 = t * TW
            if not rec:
                src = cat_s[:, t0:t0 + TW].rearrange("(k p) t -> p k t", p=128)
                P.add("sp", lambda e, src=src: e.dma_start(out=cat_v, in_=src),
                      reads=[f"cat_s_{r}_{t}" for r in range(KC)], writes=catres, dma_chan="catld")
                load_x_tile(l, t)
            for gi, (kind, c0, ncols, g0) in enumerate(wout_groups):
                got = WS.get("out", l, gi)
                if rec:
                    continue
                slot_v, slot_res = got
                for oc in range(ncols // 128):
                    co = g0 // 128 + oc
                    bank = next_bank()
                    proj_fm(slot_v, slot_res, oc, KC, cat_v, catres, TW, bank)
                    gm = ada_vec(l, 2)
                    dve_op(lambda e, bank=bank, co=co, gm=gm: e.scalar_tensor_tensor(
                        out=xt[:, co, :], in0=psum[bank][:, 0:TW], scalar=gm[:, co:co + 1], in1=xt[:, co, :],
                        op0=ALU.mult, op1=ALU.add), [f"ps{bank}", f"xt{co}", "ada"], [f"xt{co}"])
            if not rec:
                dbg_dump("dbg_x1", xt[:], [128, KC, TW], F32, [f"xt{k}" for k in range(KC)])
                rms_to_h(gs_f[:, l * KC:(l + 1) * KC], ada_vec(l, 3))
                dbg_dump("dbg_h2", hT[:], [128, KC, TW], BF16, hT_res)
            ngr = len(wup_groups) // 2
            for g in range(ngr):
                gotg = WS.get("up", l, 2 * g)
                gotv = WS.get("up", l, 2 * g + 1)
                if rec:
                    continue
                (sg_v, sg_res), (sv_v, sv_res) = gotg, gotv
                _, _, ncols, g0 = wup_groups[2 * g]
                for oc in range(ncols // 128):
                    ci = g0 // 128 + oc
                    accs = []
                    for which, (s_v, s_res, abuf, aname) in enumerate(((sg_v, sg_res, ag, "ag"), (sv_v, sv_res, av, "av"))):
                        cidx = ci + which * NFC
                        bank = next_bank()
                        proj_fm(s_v, s_res, oc, KC, hT, hT_res, TW, bank)
                        a = abuf[ci % len(abuf)]
                        ares = f"{aname}{ci % len(abuf)}"
                        wb_ = (l * 2 * NFC + cidx) * c.FW
                        bj = l * 2 * NFC + cidx
                        act_op(lambda e, bank=bank, a=a, wb_=wb_, bj=bj: e.activation(
                            out=a[:], in_=psum[bank][:, 0:TW], func=AF.Identity, scale=fdww[:, wb_ + 2:wb_ + 3],
                            bias=fdwb[:, bj:bj + 1]), [f"ps{bank}", "consts_raw"], [ares])

                        def tapfn(e, bank=bank, a=a, wb_=wb_, cidx=cidx, par=t % 2):
                            w1 = fdww[:, wb_ + 1:wb_ + 2]
                            w0 = fdww[:, wb_:wb_ + 1]
                            hnew = fhalo2[1 - par]
                            hold = fhalo2[par]
                            e.tensor_scalar(out=hnew[:, cidx, 0:2], in0=psum[bank][:, TW - 2:TW], scalar1=w0, scalar2=None, op0=ALU.mult)
                            e.scalar_tensor_tensor(out=a[:, 1:TW], in0=psum[bank][:, 0:TW - 1], scalar=w1, in1=a[:, 1:TW],
                                                   op0=ALU.mult, op1=ALU.add)
                            e.scalar_tensor_tensor(out=a[:, 2:TW], in0=psum[bank][:, 0:TW - 2], scalar=w0, in1=a[:, 2:TW],
                                                   op0=ALU.mult, op1=ALU.add)
                            e.scalar_tensor_tensor(out=hnew[:, cidx, 0:1], in0=psum[bank][:, TW - 1:TW], scalar=w1, in1=hnew[:, cidx, 0:1],
                                                   op0=ALU.mult, op1=ALU.add)
                            return e.tensor_tensor(out=a[:, 0:2], in0=a[:, 0:2], in1=hold[:, cidx, 0:2], op=ALU.add)
                        dve_op(tapfn, [f"ps{bank}", ares, "fhalo", "consts_raw"], [ares, "fhalo"])
                        accs.append((a, ares))
                    (a_g, a_gres), (a_v, a_vres) = accs
                    s = tmpf[1 + ci % 2]
                    sres = f"tmpf{1 + ci % 2}"
                    act_op(lambda e, s=s, a_g=a_g: e.activation(out=s[:], in_=a_g[:], func=AF.Silu), [a_gres], [sres])
                    dst = big[:, ci * TW:(ci + 1) * TW]
                    pool_op(lambda e, dst=dst, s=s, a_v=a_v: e.tensor_tensor(out=dst, in0=s[:], in1=a_v[:], op=ALU.mult),
                            [sres, a_vres], bigres(ci * TW, (ci + 1) * TW))
            act_v = big[:, 0:NFC * TW].rearrange("p (k t) -> p k t", t=TW)
            act_res = bigres(0, NFC * TW)
            for gi in range(len(wdn_groups)):
                got = WS.get("dn", l, gi)
                if rec:
                    continue
                slot_v, slot_res = got
                bank = next_bank()
                proj_fm(slot_v, slot_res, 0, NFC, act_v, act_res, TW, bank)
                gf = ada_vec(l, 5)
                dve_op(lambda e, bank=bank, gi=gi, gf=gf: e.scalar_tensor_tensor(
                    out=xt[:, gi, :], in0=psum[bank][:, 0:TW], scalar=gf[:, gi:gi + 1], in1=xt[:, gi, :],
                    op0=ALU.mult, op1=ALU.add), [f"ps{bank}", f"xt{gi}", "ada"], [f"xt{gi}"])
            if rec:
                continue
            xres = [f"xt{k}" for k in range(KC)]
            dbg_dump("dbg_act", big[:, 0:NFC * TW], [128, NFC * TW], BF16, act_res)
            dbg_dump("dbg_x2", xt[:], [128, KC, TW], F32, xres)
            if last_layer:
                rms_to_h(None, None, out_is_final=True, gfin=gfin_sb)
                dst = outT[:, t0:t0 + TW].rearrange("(k p) t -> p k t", p=128)
                P.add("pool", lambda e, dst=dst: e.dma_start(out=dst, in_=xt[:]),
                      reads=xres, writes=["outT"], dma_chan="xst")
            else:
                dst = xs[:, t0:t0 + TW].rearrange("(k p) t -> p k t", p=128)
                P.add("pool", lambda e, dst=dst: e.dma_start(out=dst, in_=xt[:]),
                      reads=xres, writes=[f"xs_{t}"], dma_chan="xst")

    def whole():
        rec = WS.recording
        if not rec:
            prep()
            emit_casts(0)
            pump_casts(10 ** 6)
        for l in range(L):
            if not rec and l + 1 < L:
                emit_casts(l + 1)
            ada_layer(l)
            phaseA(l)
            phaseB(l)
            phaseC(l)

    WS.recording = True
    whole()
    WS.recording = False
    whole()

    chans = sorted(P.chan_count.keys())
    sem_names = list(ENG_NAMES[:4])
    sems = {e: es.enter_context(nc.semaphore(f"s_{e}")) for e in sem_names}
    chan_sems = {ch: es.enter_context(nc.semaphore(f"c_{ch}")) for ch in chans}
    block = es.enter_context(nc.Block())
    run_engine, per_eng = P.emit(nc, sems, chan_sems, None)
    final_waits = [(chan_sems["xst"], P.chan_count["xst"])]

    @block.sync
    def _(e):
        run_engine("sp", e)

    @block.tensor
    def _(e):
        run_engine("pe", e)

    @block.scalar
    def _(e):
        run_engine("act", e)

    @block.vector
    def _(e):
        run_engine("dve", e)

    @block.gpsimd
    def _(e):
        run_engine("pool", e)
        for sem, val in final_waits:
            e.wait_ge(sem, val)
        for ch in chans:
            e.wait_ge(chan_sems[ch], P.chan_count[ch])

    es.close()
    nops = {k: len(v) for k, v in per_eng.items()}
    return nc, nops


def fm(v, nchunk_axis_last=True):
    v = np.asarray(v, dtype=np.float32)
    lead = v.shape[:-1]
    n = v.shape[-1] // 128
    r = v.reshape(lead + (n, 128))
    r = np.moveaxis(r, -1, 0)
    return np.ascontiguousarray(r.reshape(128, -1))


def make_in_maps(cfg, inputs):
    c = cfg
    L = c.L
    common = {
        "ada_w": np.ascontiguousarray(inputs["ada_w"], dtype=np.float32),
        "ada_b": fm(inputs["ada_b"]),
        "g_mix": fm(inputs["mix_norm_g"]),
        "g_ffn": fm(inputs["ffn_norm_g"]),
        "g_fin": fm(inputs["final_norm_g"]),
        "w_in": np.ascontiguousarray(inputs["w_in"], dtype=np.float32),
        "b_fg": np.ascontiguousarray(np.asarray(inputs["b_forget"], dtype=np.float32).T),
        "cdw_w": np.ascontiguousarray(np.transpose(np.asarray(inputs["conf_dw_w"], np.float32).reshape(L, c.CW, c.NCC, 128), (3, 0, 2, 1)).reshape(128, -1)),
        "cdw_b": fm(inputs["conf_dw_b"]),
        "cln_g": fm(inputs["conf_ln_g"]),
        "cln_b": fm(inputs["conf_ln_b"]),
        "sdw_w": np.ascontiguousarray(np.transpose(np.asarray(inputs["sc_dw_w"], np.float32).reshape(L, c.SW, c.NSC, 128), (3, 0, 2, 1)).reshape(128, -1)),
        "w_out": np.ascontiguousarray(inputs["w_out"], dtype=np.float32),
        "w_up": np.ascontiguousarray(inputs["w_up"], dtype=np.float32),
        "fdw_w": np.ascontiguousarray(np.transpose(np.asarray(inputs["ffn_dw_w"], np.float32).reshape(L, c.FW, 2 * c.NFC, 128), (3, 0, 2, 1)).reshape(128, -1)),
        "fdw_b": fm(inputs["ffn_dw_b"]),
        "w_down": np.ascontiguousarray(inputs["w_down"], dtype=np.float32),
        "ident": np.eye(128, dtype=np.float32),
        "tri": np.triu(np.ones((128, 128), dtype=np.float32)),
    }
    x = np.asarray(inputs["x"], dtype=np.float32)
    cc = np.asarray(inputs["c"], dtype=np.float32)
    maps = []
    for b in range(c.ncores):
        m = dict(common)
        m["xT"] = np.ascontiguousarray(x[b].T)
        m["cT"] = fm(cc[b])
        maps.append(m)
    return maps


_CACHE = {}


def run_cfg(cfg, inputs, trace=False):
    key = (cfg.D, cfg.F, cfg.L, cfg.TOK, cfg.ncores)
    if key not in _CACHE:
        _CACHE[key] = build_program(cfg)
    nc, nops = _CACHE[key]
    maps = make_in_maps(cfg, inputs)
    res = run_bass_kernel_spmd(nc, maps, core_ids=list(range(cfg.ncores)), trace=trace)
    B = cfg.ncores
    out = np.stack([np.ascontiguousarray(res.results[b]["outT"].T) for b in range(B)], axis=0)
    return out.astype(np.float32), res


def kernel(**inputs):
    cfg = Cfg()
    out, _ = run_cfg(cfg, inputs)
    return out
```

```python
import numpy as np
import concourse.bass as bass
import concourse.mybir as mybir
from concourse.bass_utils import run_bass_kernel_spmd

F32 = mybir.dt.float32
BF16 = mybir.dt.bfloat16
ALU = mybir.AluOpType
AF = mybir.ActivationFunctionType

ENG_NAMES = ("pe", "act", "dve", "pool", "sp")


class Cfg:
    def __init__(self, D=2048, F=5632, L=4, TOK=8192, TW=512, ncores=4):
        self.D, self.F, self.L, self.TOK, self.TW, self.ncores = D, F, L, TOK, TW, ncores
        self.KC = D // 128
        self.DA = D // 2
        self.NH = self.DA // 128
        self.DC = D // 4
        self.NCC = self.DC // 128
        self.DS = D - self.DA - self.DC
        self.NSC = self.DS // 128
        self.NFC = F // 128
        self.NT = TOK // TW
        self.NB = TOK // 128
        self.BPT = TW // 128
        self.IN_COLS = 3 * self.DA + self.NH + 2 * self.DC + 3 * self.DS
        self.q0, self.k0, self.v0 = 0, self.DA, 2 * self.DA
        self.f0 = 3 * self.DA
        self.cv0 = self.f0 + self.NH
        self.cg0 = self.cv0 + self.DC
        self.sx0 = self.cg0 + self.DC
        self.sb0 = self.sx0 + self.DS
        self.sc0 = self.sb0 + self.DS
        self.CW, self.SW, self.FW = 31, 3, 3


class Res:
    __slots__ = ("name", "writer", "readers")

    def __init__(self, name):
        self.name = name
        self.writer = None
        self.readers = []


class Op:
    __slots__ = ("eng", "fn", "deps", "signal", "sigidx", "chan", "count", "is_dma", "small")

    def __init__(self, eng, fn, is_dma=False, chan=None):
        self.eng, self.fn, self.is_dma, self.chan = eng, fn, is_dma, chan
        self.deps = []
        self.signal = False
        self.sigidx = 0
        self.count = 0


class Prog:
    def __init__(self):
        self.ops = []
        self.chan_count = {}
        self.res = {}
        self.bulk = set()

    def R(self, name):
        r = self.res.get(name)
        if r is None:
            r = self.res[name] = Res(name)
        return r

    def _rl(self, xs):
        out = []
        for x in xs:
            out.append(self.R(x) if isinstance(x, str) else x)
        return out

    def add(self, eng, fn, reads=(), writes=(), dma_chan=None, small=False):
        op = Op(eng, fn, is_dma=dma_chan is not None, chan=dma_chan)
        op.small = small
        deps = set()
        for r in self._rl(reads):
            if r.writer is not None:
                deps.add(r.writer)
            r.readers.append(op)
        for w in self._rl(writes):
            if w.writer is not None and not (dma_chan is not None and w.writer.is_dma and w.writer.chan == dma_chan):
                deps.add(w.writer)
            for rd in w.readers:
                if rd is not op:
                    deps.add(rd)
            w.writer = op
            w.readers = []
        for d in deps:
            if d is op:
                continue
            if (not d.is_dma) and d.eng == eng and not op.is_dma and not d.small:
                continue
            op.deps.append(d)
            d.signal = True
        if dma_chan is not None:
            c = self.chan_count.get(dma_chan, 0) + 16
            self.chan_count[dma_chan] = c
            op.count = c
        self.ops.append(op)
        return op

    def emit(self, nc, sems, chan_sems, engines):
        cnt = {e: 0 for e in ENG_NAMES}
        for op in self.ops:
            if not op.is_dma and op.signal:
                cnt[op.eng] += 1
                op.sigidx = cnt[op.eng]
        per_eng = {e: [] for e in ENG_NAMES}
        for op in self.ops:
            per_eng[op.eng].append(op)

        def run_engine(ename, eobj):
            waited = {}
            for op in per_eng[ename]:
                need = {}
                for d in op.deps:
                    if d.is_dma:
                        key, val = ("c", d.chan), (self.chan_count[d.chan] if d.chan in self.bulk else d.count)
                    else:
                        key, val = ("e", d.eng), d.sigidx
                    if val > need.get(key, 0):
                        need[key] = val
                for key, val in need.items():
                    if waited.get(key, 0) >= val:
                        continue
                    sem = chan_sems[key[1]] if key[0] == "c" else sems[key[1]]
                    eobj.wait_ge(sem, val)
                    waited[key] = val
                ins = op.fn(eobj)
                if op.is_dma:
                    ins.then_inc(chan_sems[op.chan], 16)
                elif op.signal:
                    ins.then_inc(sems[op.eng], 1)

        return run_engine, per_eng


def build_program(cfg, debug_outs=False):
    c = cfg
    D, F, L, TOK, TW, KC, NH, NCC, NSC, NFC, NT, NB, BPT = (
        c.D, c.F, c.L, c.TOK, c.TW, c.KC, c.NH, c.NCC, c.NSC, c.NFC, c.NT, c.NB, c.BPT)
    DA, DC, DS = c.DA, c.DC, c.DS
    nc = bass.Bass("TRN2", target_bir_lowering=False)
    P = Prog()

    def din(name, shape, dt=F32):
        return nc.dram_tensor(name, list(shape), dt, kind="ExternalInput").ap()

    def dscr(name, shape, dt):
        kind = "ExternalOutput" if (debug_outs and not name.startswith("wb_")) else "Internal"
        return nc.dram_tensor(name, list(shape), dt, kind=kind).ap()

    xT = din("xT", [D, TOK])
    cT = din("cT", [128, KC])
    ada_w = din("ada_w", [L, D, 6 * D])
    ada_b = din("ada_b", [128, L * 6 * KC])
    g_mix = din("g_mix", [128, L * KC])
    g_ffn = din("g_ffn", [128, L * KC])
    g_fin = din("g_fin", [128, KC])
    w_in = din("w_in", [L, D, c.IN_COLS])
    b_fg = din("b_fg", [NH, L])
    cdw_w = din("cdw_w", [128, L * NCC * c.CW])
    cdw_b = din("cdw_b", [128, L * NCC])
    cln_g = din("cln_g", [128, L * NCC])
    cln_b = din("cln_b", [128, L * NCC])
    sdw_w = din("sdw_w", [128, L * NSC * c.SW])
    w_out = din("w_out", [L, D, D])
    w_up = din("w_up", [L, D, 2 * F])
    fdw_w = din("fdw_w", [128, L * 2 * NFC * c.FW])
    fdw_b = din("fdw_b", [128, L * 2 * NFC])
    w_down = din("w_down", [L, F, D])
    ident_in = din("ident", [128, 128])
    tri_in = din("tri", [128, 128])
    outT = nc.dram_tensor("outT", [D, TOK], F32, kind="ExternalOutput").ap()

    xs = dscr("xs", [D, TOK], F32)
    qT_s = dscr("qT_s", [DA, TOK], BF16)
    kT_s = dscr("kT_s", [DA, TOK], BF16)
    v_s = dscr("v_s", [TOK, DA], BF16)
    Fb_s = dscr("Fb_s", [NH, NB], F32)
    F_s = dscr("F_s", [NH, TOK], F32)
    cat_s = dscr("cat_s", [D, TOK], BF16)

    GC = 512

    def groups_of(n):
        return [(c0, min(GC, n - c0)) for c0 in range(0, n, GC)]

    win_groups = []
    for nm, c0, n in (("q", c.q0, DA), ("k", c.k0, DA), ("v", c.v0, DA), ("f", c.f0, NH),
                      ("cg", c.cg0, DC), ("cv", c.cv0, DC), ("sx", c.sx0, DS),
                      ("sc", c.sc0, DS), ("sb", c.sb0, DS)):
        for (g0, gn) in groups_of(n):
            win_groups.append((nm, c0 + g0, gn, g0))
    wout_groups = [("o", g0, gn, g0) for (g0, gn) in groups_of(D)]
    wup_groups = []
    for (g0, gn) in groups_of(F):
        wup_groups.append(("ug", g0, gn, g0))
        wup_groups.append(("uv", F + g0, gn, g0))
    wdn_groups = [("d", oc * 128, 128, oc * 128) for oc in range(KC)]
    ada_groups = [("a", g0, gn, g0) for (g0, gn) in groups_of(6 * D)]

    SLOT_ELEMS = max(KC * GC, NFC * 128)
    wb = {}
    for l in range(L):
        wb["in", l] = dscr(f"wb_in{l}", [len(win_groups), 128, KC * GC], BF16)
        wb["out", l] = dscr(f"wb_out{l}", [len(wout_groups), 128, KC * GC], BF16)
        wb["up", l] = dscr(f"wb_up{l}", [len(wup_groups), 128, KC * GC], BF16)
        wb["dn", l] = dscr(f"wb_dn{l}", [len(wdn_groups), 128, NFC * 128], BF16)
        wb["ada", l] = dscr(f"wb_ada{l}", [len(ada_groups), 128, KC * GC], BF16)
    wsrc = {"in": w_in, "out": w_out, "up": w_up, "dn": w_down, "ada": ada_w}
    wgroups = {"in": win_groups, "out": wout_groups, "up": wup_groups, "dn": wdn_groups,
               "ada": ada_groups}
    wkc = {"in": KC, "out": KC, "up": KC, "dn": NFC, "ada": KC}

    import contextlib
    es = contextlib.ExitStack()

    def sb(name, shape, dt=F32):
        return es.enter_context(nc.sbuf_tensor(name, list(shape), dt))

    NSLOT = 3
    ring = [sb(f"ring{i}", [128, SLOT_ELEMS], BF16) for i in range(NSLOT)]
    xt = sb("xt", [128, KC, TW])
    hTb = [sb(f"hT{i}", [128, KC, TW], BF16) for i in range(2)]
    hT = hTb[0]
    sq = [sb(f"sq{i}", [128, TW], BF16) for i in range(2)]
    tmpf = [sb(f"tmpf{i}", [128, TW]) for i in range(3)]
    rstd_b = sb("rstd_b", [128, TW])
    MCS = max(NCC, NSC)
    PA_SIZES = [MCS * TW, MCS * TW, NCC * (c.CW - 1 + TW), NSC * (c.SW - 1 + TW), TW, TW]
    PA_F32 = sum(PA_SIZES)
    BIGN = max(NFC * TW, 2 * TOK, 2 * PA_F32)
    BIGN = (BIGN + 511) // 512 * 512
    big = sb("big", [128, BIGN], BF16)
    bigf = big[:, 0:2 * PA_F32].bitcast(F32)
    _o = [0]

    def carve(n):
        v = bigf[:, _o[0]:_o[0] + n]
        _o[0] += n
        return v
    bufA = carve(PA_SIZES[0]).rearrange("p (c t) -> p c t", t=TW)
    accb = carve(PA_SIZES[1]).rearrange("p (c t) -> p c t", t=TW)
    glu = carve(PA_SIZES[2]).rearrange("p (c t) -> p c t", t=c.CW - 1 + TW)
    tbuf = carve(PA_SIZES[3]).rearrange("p (c t) -> p c t", t=c.SW - 1 + TW)
    lnm = carve(TW)
    lnv = carve(TW)
    PA_NAMES = ([f"bufA{i}" for i in range(MCS)] + [f"accb{i}" for i in range(MCS)] + [f"glu{i}" for i in range(NCC)]
                + [f"tbuf{i}" for i in range(NSC)] + ["glu_h", "tbuf_h", "lnm", "lnv"])
    stage = [sb(f"stage{i}", [128, TW], BF16) for i in range(4)]
    bar_t = sb("bar_t", [128, 2])
    qbuf = [sb(f"qbuf{i}", [128, TW], BF16) for i in range(2)]
    bias4 = [sb(f"bias4_{i}", [128, BPT, NB]) for i in range(2)]
    fsp = tmpf[0][0:NH, :]
    pbuf = [sb(f"pbuf{i}", [128, TW], BF16) for i in range(4)]
    ft = sb("ft", [NH, TW])
    fprev = sb("fprev", [NH, 1])
    fb4 = sb("fb4", [NH, BPT])
    fones = sb("fones", [NH, TW])
    fcol = sb("fcol", [128, NB, NH])
    ft0b = sb("ft0b", [128, NH * NB])
    fhalo2 = [sb(f"fhalo2_{i}", [128, 2 * NFC, 2]) for i in range(2)]
    ag = [sb(f"ag{i}", [128, TW]) for i in range(1)]
    av = [sb(f"av{i}", [128, TW]) for i in range(1)]
    c_sb = sb("c_sb", [128, KC])
    cact = sb("cact", [128, KC], BF16)
    adab_sb = sb("adab_sb", [128, L * 6 * KC])
    ada_sb = sb("ada_sb", [128, L * 6 * KC])
    gmix_sb = sb("gmix_sb", [128, L * KC])
    gffn_sb = sb("gffn_sb", [128, L * KC])
    gfin_sb = sb("gfin_sb", [128, KC])
    gs_m = sb("gs_m", [128, L * KC])
    gs_f = sb("gs_f", [128, L * KC])
    bfg_sb = sb("bfg_sb", [NH, L])
    negb = sb("negb", [NH, L])
    cdww = sb("cdww", [128, L * NCC * c.CW])
    cdwb = sb("cdwb", [128, L * NCC])
    clng = sb("clng", [128, L * NCC])
    clnb = sb("clnb", [128, L * NCC])
    sdww = sb("sdww", [128, L * NSC * c.SW])
    fdww = sb("fdww", [128, L * 2 * NFC * c.FW])
    fdwb = sb("fdwb", [128, L * 2 * NFC])
    ident = sb("ident_sb", [128, 128])
    tri_f = sb("tri_f", [128, 128])
    tri = sb("tri_b", [128, 128], BF16)
    ident_bf = sb("ident_bf", [128, 128], BF16)
    ones_bf = sb("ones_bf", [128, 128], BF16)
    ones_f = sb("ones_f", [128, 128])
    psum = [es.enter_context(nc.psum_tensor(f"ps{i}", [128, 512], F32)) for i in range(8)]

    cat_off = BIGN - KC * TW
    cat_v = big[:, cat_off:cat_off + KC * TW].rearrange("p (k t) -> p k t", t=TW)
    K_v = big[:, 0:TOK]
    V_v = big[:, TOK:2 * TOK].rearrange("p (b d) -> p b d", d=128)

    def bigres(lo, hi):
        return [f"big{i}" for i in range(lo // 512, (hi + 511) // 512)]

    INV_SQRT_DH = 1.0 / np.sqrt(128.0)

    class WStream:
        def __init__(self):
            self.seq = []
            self.recording = True
            self.pos = 0
            self.loaded = 0

        def _issue(self, i):
            kind, l, gi = self.seq[i]
            slot = i % NSLOT
            _, c0, ncols, _ = wgroups[kind][gi]
            kcw = wkc[kind]
            n = kcw * (ncols if kind != "dn" else 128)
            src = wb[kind, l][gi][:, 0:n]
            dst = ring[slot][:, 0:n]
            P.add("sp", lambda e, dst=dst, src=src: e.dma_start(out=dst, in_=src),
                  reads=[f"wb_{kind}{l}_{gi}"], writes=[f"ring{slot}"], dma_chan=f"ring{slot}")

        def get(self, kind, l, gi):
            if self.recording:
                self.seq.append((kind, l, gi))
                return None
            i = self.pos
            assert self.seq[i] == (kind, l, gi)
            self.pos += 1
            while self.loaded < min(len(self.seq), i + NSLOT - 1):
                self._issue(self.loaded)
                self.loaded += 1
            slot = i % NSLOT
            _, c0, ncols, _ = wgroups[kind][gi]
            kcw = wkc[kind]
            nn = ncols if kind != "dn" else 128
            view = ring[slot][:, 0:kcw * nn].rearrange("p (k n) -> p k n", n=nn)
            return view, f"ring{slot}"

    WS = WStream()
    dbg = {}

    def dbg_dump(name, src_ap, shape, dt, reads):
        if not debug_outs or name in dbg or WS.recording:
            return
        t = nc.dram_tensor(name, list(shape), dt, kind="ExternalOutput").ap()
        dbg[name] = t
        P.add("pool", lambda e: e.dma_start(out=t, in_=src_ap), reads=reads, writes=[name], dma_chan="dbg_" + name)

    state = {"mm": 0}

    def next_bank(nb=6):
        b = state["mm"] % nb
        state["mm"] += 1
        return b

    cast_q = []

    def pump_casts(n):
        for _ in range(min(n, len(cast_q))):
            cast_q.pop(0)()

    def emit_casts(l):
        for kind in ("ada", "in", "out", "up", "dn"):
            P.bulk.add(f"cast_{kind}{l}")
            src_all = wsrc[kind][l]
            kcw = wkc[kind]
            for gi, (_, c0, ncols, _) in enumerate(wgroups[kind]):
                src = src_all[:, c0:c0 + ncols].rearrange("(k p) n -> p k n", p=128)
                dst = wb[kind, l][gi][:, 0:kcw * ncols].rearrange("p (k n) -> p k n", n=ncols)
                cast_q.append(lambda dst=dst, src=src, kind=kind, gi=gi: P.add(
                    "pool", lambda e: e.dma_start(out=dst, in_=src, max_dma_last_dim=8192),
                    reads=[], writes=[f"wb_{kind}{l}_{gi}"], dma_chan=f"cast_{kind}{l}"))

    def act_op(fn, reads, writes):
        return P.add("act", fn, reads, writes)

    def dve_op(fn, reads, writes, small=False):
        return P.add("dve", fn, reads, writes, small=small)

    def pool_op(fn, reads, writes):
        return P.add("pool", fn, reads, writes)

    def pe_op(fn, reads, writes):
        return P.add("pe", fn, reads, writes)

    def proj_fm(slot_v, slot_res, oc, kcw, rhs_tile, rhs_res, n, bank):
        def fn(e):
            ins = None
            for k in range(kcw):
                ins = e.matmul(psum[bank][:, 0:n], slot_v[:, k, oc * 128:(oc + 1) * 128],
                               rhs_tile[:, k, 0:n], start=(k == 0), stop=(k == kcw - 1))
            return ins
        pe_op(fn, reads=[slot_res] + rhs_res, writes=[f"ps{bank}"])

    def load_x_tile(l, t):
        src = (xT if l == 0 else xs)[:, t * TW:(t + 1) * TW].rearrange("(k p) t -> p k t", p=128)
        half = KC // 2
        for hf in range(2):
            P.add("sp", lambda e, hf=hf: e.dma_start(out=xt[:, hf * half:(hf + 1) * half, :],
                                                      in_=src[:, hf * half:(hf + 1) * half, :]),
                  reads=([f"xs_{t}"] if l > 0 else []), writes=[f"xt{k}" for k in range(hf * half, (hf + 1) * half)],
                  dma_chan=f"xt{hf}")

    def rms_to_h(gs_ap, sh_ap, out_is_final=False, gfin=None, hb=0):
        hT = hTb[hb]
        SB = 6
        for k in range(KC):
            s = sq[k % 2]
            act_op(lambda e, k=k, s=s: e.activation(out=s[:], in_=xt[:, k, :], func=AF.Square),
                   reads=[f"xt{k}"], writes=[f"sq{k % 2}"])
            pe_op(lambda e, k=k, s=s: e.matmul(psum[SB][:, 0:TW], ones_bf[:], s[:], start=(k == 0), stop=(k == KC - 1)),
                  reads=[f"sq{k % 2}", "consts"], writes=[f"ps{SB}"])
        act_op(lambda e: e.activation(out=rstd_b[:], in_=psum[SB][:, 0:TW], func=AF.Sqrt, scale=1.0 / D, bias=eps_rms[:, 0:1]),
               reads=[f"ps{SB}", "consts"], writes=["rstd_b"])
        dve_op(lambda e: e.reciprocal(out=rstd_b[:], in_=rstd_b[:]), reads=["rstd_b"], writes=["rstd_b"])
        for k in range(KC):
            if out_is_final:
                dve_op(lambda e, k=k: e.scalar_tensor_tensor(out=xt[:, k, :], in0=xt[:, k, :], scalar=gfin[:, k:k + 1],
                                                             in1=rstd_b[:], op0=ALU.mult, op1=ALU.mult),
                       reads=[f"xt{k}", "rstd_b", "consts"], writes=[f"xt{k}"])
            else:
                tf = tmpf[k % 3]
                dve_op(lambda e, k=k, tf=tf: e.tensor_tensor(out=tf[:], in0=xt[:, k, :], in1=rstd_b[:], op=ALU.mult),
                       reads=[f"xt{k}", "rstd_b"], writes=[f"tmpf{k % 3}"])
                act_op(lambda e, k=k, tf=tf: e.activation(out=hT[:, k, :], in_=tf[:], func=AF.Identity,
                                                          scale=gs_ap[:, k:k + 1], bias=sh_ap[:, k:k + 1]),
                       reads=[f"tmpf{k % 3}", "ada"], writes=[f"hT{hb}_{k}"])

    hT_res = [f"hT0_{k}" for k in range(KC)]
    hT_resb = [[f"hT{b}_{k}" for k in range(KC)] for b in range(2)]

    eps_rms = sb("eps_rms", [128, 1])
    eps_ln = sb("eps_ln", [128, 1])
    one_c = sb("one_c", [128, 1])

    def prep():
        loads = [(c_sb, cT), (adab_sb, ada_b), (gmix_sb, g_mix), (gffn_sb, g_ffn), (gfin_sb, g_fin),
                 (bfg_sb, b_fg), (cdww, cdw_w), (cdwb, cdw_b), (clng, cln_g), (clnb, cln_b),
                 (sdww, sdw_w), (fdww, fdw_w), (fdwb, fdw_b), (ident, ident_in), (tri_f, tri_in)]
        for i, (dst, src) in enumerate(loads):
            P.add("sp", lambda e, dst=dst, src=src: e.dma_start(out=dst[:], in_=src),
                  reads=[], writes=["consts_raw"], dma_chan="consts")
        dve_op(lambda e: e.memset(ones_f[:], 1.0), [], ["consts0"], small=True)
        dve_op(lambda e: e.memset(eps_rms[:], 1e-6), [], ["consts0"], small=True)
        dve_op(lambda e: e.memset(eps_ln[:], 1e-5), [], ["consts0"], small=True)
        dve_op(lambda e: e.memset(one_c[:], 1.0), [], ["consts0"], small=True)
        dve_op(lambda e: e.memset(fones[:], 1.0), [], ["consts0"], small=True)
        dve_op(lambda e: e.tensor_copy(out=ones_bf[:], in_=ones_f[:]), ["consts0"], ["consts"], small=True)
        dve_op(lambda e: e.tensor_scalar(out=tri[:], in0=tri_f[:], scalar1=-1.0, scalar2=30000.0, op0=ALU.add, op1=ALU.mult),
               ["consts_raw"], ["consts"], small=True)
        dve_op(lambda e: e.tensor_copy(out=ident_bf[:], in_=ident[:]), ["consts_raw"], ["consts"], small=True)
        dve_op(lambda e: e.tensor_scalar(out=negb[:], in0=bfg_sb[:], scalar1=-1.0, scalar2=None, op0=ALU.mult),
               ["consts_raw"], ["consts"], small=True)
        act_op(lambda e: e.activation(out=cact[:], in_=c_sb[:], func=AF.Silu), ["consts_raw"], ["consts"])

    def ada_layer(l):
        AB = 7
        n6 = 6 * KC
        for gi, (_, c0, ncols, _) in enumerate(ada_groups):
            got = WS.get("ada", l, gi)
            if got is None:
                continue
            slot_v, slot_res = got
            for oc in range(ncols // 128):
                col = c0 // 128 + oc

                def fn(e, oc=oc, col=col, slot_v=slot_v):
                    ins = None
                    for k in range(KC):
                        ins = e.matmul(psum[AB][:, col:col + 1], slot_v[:, k, oc * 128:(oc + 1) * 128],
                                       cact[:, k:k + 1], start=(k == 0), stop=(k == KC - 1))
                    return ins
                pe_op(fn, reads=[slot_res, "consts"], writes=[f"ps{AB}"])
        if WS.recording:
            return
        dve_op(lambda e: e.tensor_tensor(out=ada_sb[:, l * n6:(l + 1) * n6], in0=psum[AB][:, 0:n6],
                                         in1=adab_sb[:, l * n6:(l + 1) * n6], op=ALU.add),
               [f"ps{AB}", "consts_raw"], ["ada"], small=True)
        a0 = l * n6
        dve_op(lambda e: e.scalar_tensor_tensor(out=gs_m[:, l * KC:(l + 1) * KC], in0=ada_sb[:, a0 + KC:a0 + 2 * KC],
                                                scalar=1.0, in1=gmix_sb[:, l * KC:(l + 1) * KC], op0=ALU.add, op1=ALU.mult),
               ["ada", "consts_raw"], ["ada"], small=True)
        dve_op(lambda e: e.scalar_tensor_tensor(out=gs_f[:, l * KC:(l + 1) * KC], in0=ada_sb[:, a0 + 4 * KC:a0 + 5 * KC],
                                                scalar=1.0, in1=gffn_sb[:, l * KC:(l + 1) * KC], op0=ALU.add, op1=ALU.mult),
               ["ada", "consts_raw"], ["ada"], small=True)

    def ada_vec(l, which):
        dbg_dump("dbg_ada", ada_sb[:], [128, L * 6 * KC], F32, ["ada"])
        dbg_dump("dbg_cact", cact[:], [128, KC], BF16, ["consts"])
        a0 = l * 6 * KC + which * KC
        return ada_sb[:, a0:a0 + KC]

    def barrier_big():
        dve_op(lambda e: e.memset(bar_t[:], 0.0), [], PA_NAMES + bigres(0, BIGN), small=True)

    def phaseA(l):
        rec = WS.recording
        if not rec:
            barrier_big()
            for ci in range(NCC):
                dve_op(lambda e, ci=ci: e.memset(glu[:, ci, 0:c.CW - 1], 0.0), [], ["glu_h"], small=True)
            for ci in range(NSC):
                dve_op(lambda e, ci=ci: e.memset(tbuf[:, ci, 0:c.SW - 1], 0.0), [], ["tbuf_h"], small=True)
            dve_op(lambda e: e.memset(fprev[:], 0.0), [], ["fprev"], small=True)
            load_x_tile(l, 0)
        sstate = {"st": 0, "vs": 0}
        NORM_AT = min(3, len(win_groups) - 1)
        if not rec:
            rms_to_h(gs_m[:, l * KC:(l + 1) * KC], ada_vec(l, 0), hb=0)
            if NT > 1:
                load_x_tile(l, 1)
        for t in range(NT):
            t0 = t * TW
            hT = hTb[t % 2]
            hT_res = hT_resb[t % 2]
            for gi, (kind, c0, ncols, g0) in enumerate(win_groups):
                got = WS.get("in", l, gi)
                if rec:
                    continue
                if gi == NORM_AT and t + 1 < NT:
                    rms_to_h(gs_m[:, l * KC:(l + 1) * KC], ada_vec(l, 0), hb=(t + 1) % 2)
                    if t + 2 < NT:
                        load_x_tile(l, t + 2)
                slot_v, slot_res = got
                if kind == "v":
                    for tb in range(BPT):
                        bank = next_bank()

                        def fn(e, tb=tb, bank=bank, slot_v=slot_v, ncols=ncols, hT=hT):
                            ins = None
                            for k in range(KC):
                                ins = e.matmul(psum[bank][:, 0:ncols], hT[:, k, tb * 128:(tb + 1) * 128],
                                               slot_v[:, k, 0:ncols], start=(k == 0), stop=(k == KC - 1))
                            return ins
                        pe_op(fn, reads=[slot_res] + hT_res, writes=[f"ps{bank}"])
                        vi = sstate["st"] % 4
                        sstate["st"] += 1
                        vsb = stage[vi]
                        act_op(lambda e, bank=bank, vsb=vsb, ncols=ncols: e.activation(out=vsb[:, 0:ncols], in_=psum[bank][:, 0:ncols], func=AF.Copy),
                               [f"ps{bank}"], [f"stage{vi}"])
                        dst = v_s[t0 + tb * 128:t0 + (tb + 1) * 128, g0:g0 + ncols]
                        P.add("pool", lambda e, dst=dst, vsb=vsb, ncols=ncols: e.dma_start(out=dst, in_=vsb[:, 0:ncols]),
                              reads=[f"stage{vi}"], writes=[f"v_s_{t}_{tb}_{g0 // GC}"], dma_chan=f"stage{vi}")
                    continue
                if kind == "f":
                    bank = next_bank()

                    def fn(e, bank=bank, slot_v=slot_v, hT=hT):
                        ins = None
                        for k in range(KC):
                            ins = e.matmul(psum[bank][0:NH, 0:TW], slot_v[:, k, 0:NH], hT[:, k, :],
                                           start=(k == 0), stop=(k == KC - 1))
                        return ins
                    pe_op(fn, reads=[slot_res] + hT_res, writes=[f"ps{bank}"])
                    act_op(lambda e, bank=bank: e.activation(out=fsp[:], in_=psum[bank][0:NH, 0:TW], func=AF.Exp,
                                                             scale=-1.0, bias=negb[:, l:l + 1]),
                           [f"ps{bank}", "consts"], ["tmpf0"])
                    act_op(lambda e: e.activation(out=fsp[:], in_=fsp[:], func=AF.Ln, scale=1.0, bias=one_c[0:NH, 0:1]),
                           ["tmpf0", "consts0"], ["tmpf0"])
                    dve_op(lambda e: e.tensor_tensor_scan(out=ft[:], data0=fones[:], data1=fsp[:], initial=fprev[:, 0:1],
                                                          op0=ALU.mult, op1=ALU.subtract),
                           ["tmpf0", "fprev", "consts0"], ["ft"], small=True)
                    dve_op(lambda e: e.tensor_copy(out=fprev[:], in_=ft[:, TW - 1:TW]), ["ft"], ["fprev"], small=True)
                    P.add("pool", lambda e, t0=t0: e.dma_start(out=F_s[:, t0:t0 + TW], in_=ft[:]),
                          reads=["ft"], writes=[f"F_s_{t}"], dma_chan="frow_st")
                    src = ft[:, 0:TW].rearrange("h (b p) -> h b p", p=128)[:, :, 0]
                    dve_op(lambda e, src=src: e.tensor_copy(out=fb4[:], in_=src), ["ft"], ["fb4"], small=True)
                    P.add("pool", lambda e, t=t: e.dma_start(out=Fb_s[:, t * BPT:(t + 1) * BPT], in_=fb4[:]),
                          reads=["fb4"], writes=[f"Fb_s_{t}"], dma_chan="fb")
                    for bb in range(BPT):
                        tbank = next_bank()
                        pe_op(lambda e, bb=bb, tbank=tbank: e.transpose(out=psum[tbank][:, 0:NH], in_=ft[0:NH, bb * 128:(bb + 1) * 128],
                                                                         identity=ident[0:NH, 0:NH]),
                              ["ft", "consts_raw"], [f"ps{tbank}"])
                        kb = t * BPT + bb
                        dve_op(lambda e, tbank=tbank, kb=kb: e.tensor_copy(out=fcol[:, kb, :], in_=psum[tbank][:, 0:NH]),
                               [f"ps{tbank}"], ["fcol"])
                    continue
                for oc in range(ncols // 128):
                    ci = (g0 // 128) + oc
                    bank = next_bank()
                    proj_fm(slot_v, slot_res, oc, KC, hT, hT_res, TW, bank)
                    pr = f"ps{bank}"
                    if kind in ("q", "k"):
                        si = sstate["st"] % 4
                        sstate["st"] += 1
                        stg = stage[si]
                        if kind == "q":
                            act_op(lambda e, bank=bank, stg=stg: e.activation(out=stg[:], in_=psum[bank][:, 0:TW], func=AF.Copy),
                                   [pr], [f"stage{si}"])
                        else:
                            dve_op(lambda e, bank=bank, stg=stg: e.tensor_copy(out=stg[:], in_=psum[bank][:, 0:TW]),
                                   [pr], [f"stage{si}"])
                        dstT = (qT_s if kind == "q" else kT_s)[ci * 128:(ci + 1) * 128, t0:t0 + TW]
                        P.add("pool", lambda e, dstT=dstT, stg=stg: e.dma_start(out=dstT, in_=stg[:]),
                              reads=[f"stage{si}"], writes=[("qT_s" if kind == "q" else "kT_s") + f"_{ci}_{t}"], dma_chan=f"stage{si}")
                    elif kind == "cg":
                        act_op(lambda e, bank=bank, ci=ci: e.activation(out=bufA[:, ci, :], in_=psum[bank][:, 0:TW], func=AF.Sigmoid),
                               [pr], [f"bufA{ci}"])
                    elif kind == "cv":
                        H = c.CW - 1
                        dve_op(lambda e, bank=bank, ci=ci, H=H: e.tensor_tensor(out=glu[:, ci, H:H + TW], in0=psum[bank][:, 0:TW],
                                                                           in1=bufA[:, ci, :], op=ALU.mult),
                               [pr, f"bufA{ci}"], [f"glu{ci}"])
                        wbase = (l * NCC + ci) * c.CW

                        def convfn(e, ci=ci, wbase=wbase, H=H):
                            ins = e.tensor_scalar(out=accb[:, ci, :], in0=glu[:, ci, H:H + TW],
                                                  scalar1=cdww[:, wbase + H:wbase + H + 1],
                                                  scalar2=cdwb[:, l * NCC + ci:l * NCC + ci + 1], op0=ALU.mult, op1=ALU.add)
                            for kk in range(H):
                                ins = e.scalar_tensor_tensor(out=accb[:, ci, :], in0=glu[:, ci, kk:kk + TW],
                                                             scalar=cdww[:, wbase + kk:wbase + kk + 1], in1=accb[:, ci, :],
                                                             op0=ALU.mult, op1=ALU.add)
                            return ins
                        dve_op(convfn, [f"glu{ci}", "glu_h", "consts_raw"], [f"accb{ci}"])
                        act_op(lambda e, ci=ci, H=H: e.activation(out=glu[:, ci, 0:H], in_=glu[:, ci, TW:TW + H], func=AF.Copy),
                               [f"glu{ci}", f"accb{ci}"], ["glu_h"])
                        if ci == NCC - 1:
                            conf_ln_out(l, t0)
                    elif kind == "sx":
                        act_op(lambda e, bank=bank, ci=ci: e.activation(out=bufA[:, ci, :], in_=psum[bank][:, 0:TW], func=AF.Copy),
                               [pr], [f"bufA{ci}"])
                    elif kind == "sc":
                        H = c.SW - 1
                        dve_op(lambda e, bank=bank, ci=ci, H=H: e.tensor_tensor(out=tbuf[:, ci, H:H + TW], in0=psum[bank][:, 0:TW],
                                                                                 in1=bufA[:, ci, :], op=ALU.mult),
                               [pr, f"bufA{ci}"], [f"tbuf{ci}"])
                        wbase = (l * NSC + ci) * c.SW

                        def sconvfn(e, ci=ci, wbase=wbase, H=H):
                            ins = e.tensor_scalar(out=accb[:, ci, :], in0=tbuf[:, ci, H:H + TW],
                                                  scalar1=sdww[:, wbase + H:wbase + H + 1], scalar2=None, op0=ALU.mult)
                            for kk in range(H):
                                ins = e.scalar_tensor_tensor(out=accb[:, ci, :], in0=tbuf[:, ci, kk:kk + TW],
                                                             scalar=sdww[:, wbase + kk:wbase + kk + 1], in1=accb[:, ci, :],
                                                             op0=ALU.mult, op1=ALU.add)
                            return ins
                        dve_op(sconvfn, [f"tbuf{ci}", "tbuf_h", "consts_raw"], [f"accb{ci}"])
                        act_op(lambda e, ci=ci, H=H: e.activation(out=tbuf[:, ci, 0:H], in_=tbuf[:, ci, TW:TW + H], func=AF.Copy),
                               [f"tbuf{ci}", f"accb{ci}"], ["tbuf_h"])
                    elif kind == "sb":
                        si = sstate["st"] % 4
                        sstate["st"] += 1
                        stg = stage[si]
                        dve_op(lambda e, bank=bank, ci=ci, stg=stg: e.tensor_tensor(out=stg[:], in0=psum[bank][:, 0:TW],
                                                                                    in1=accb[:, ci, :], op=ALU.mult),
                               [pr, f"accb{ci}"], [f"stage{si}"])
                        r0 = DA + DC + ci * 128
                        dstT = cat_s[r0:r0 + 128, t0:t0 + TW]
                        P.add("pool", lambda e, dstT=dstT, stg=stg: e.dma_start(out=dstT, in_=stg[:]),
                              reads=[f"stage{si}"], writes=[f"cat_s_{r0 // 128}_{t}"], dma_chan=f"stage{si}")

        def _unused():
            pass

    def conf_ln_out(l, t0):
        MB, VB = 6, 7
        for ci in range(NCC):
            pe_op(lambda e, ci=ci: e.matmul(psum[MB][:, 0:TW], ones_f[:], accb[:, ci, :], start=(ci == 0), stop=(ci == NCC - 1)),
                  [f"accb{ci}", "consts0"], [f"ps{MB}"])
        for ci in range(NCC):
            tf = tmpf[ci % 3]
            act_op(lambda e, ci=ci, tf=tf: e.activation(out=tf[:], in_=accb[:, ci, :], func=AF.Square),
                   [f"accb{ci}"], [f"tmpf{ci % 3}"])
            pe_op(lambda e, ci=ci, tf=tf: e.matmul(psum[VB][:, 0:TW], ones_f[:], tf[:], start=(ci == 0), stop=(ci == NCC - 1)),
                  [f"tmpf{ci % 3}", "consts0"], [f"ps{VB}"])
        dbg_dump("dbg_accb", accb[:], [128, max(NCC, NSC), TW], F32, [f"accb{ci}" for ci in range(NCC)])
        dbg_dump("dbg_glu", glu[:], [128, NCC, c.CW - 1 + TW], F32, [f"glu{ci}" for ci in range(NCC)])
        act_op(lambda e: e.activation(out=lnm[:], in_=psum[MB][:, 0:TW], func=AF.Copy, scale=1.0 / DC),
               [f"ps{MB}"], ["lnm"])
        dbg_dump("dbg_lnm", lnm[:], [128, TW], F32, ["lnm"])
        dve_op(lambda e: e.tensor_tensor(out=lnv[:], in0=lnm[:], in1=lnm[:], op=ALU.mult), ["lnm"], ["lnv"])
        dve_op(lambda e: e.scalar_tensor_tensor(out=lnv[:], in0=psum[VB][:, 0:TW], scalar=1.0 / DC, in1=lnv[:],
                                                op0=ALU.mult, op1=ALU.subtract),
               [f"ps{VB}", "lnv"], ["lnv"])
        act_op(lambda e: e.activation(out=lnv[:], in_=lnv[:], func=AF.Sqrt, scale=1.0, bias=eps_ln[:, 0:1]),
               ["lnv", "consts0"], ["lnv"])
        dve_op(lambda e: e.reciprocal(out=lnv[:], in_=lnv[:]), ["lnv"], ["lnv"])
        dbg_dump("dbg_lnv", lnv[:], [128, TW], F32, ["lnv"])
        for ci in range(NCC):
            dve_op(lambda e, ci=ci: e.tensor_tensor(out=accb[:, ci, :], in0=accb[:, ci, :], in1=lnm[:], op=ALU.subtract),
                   [f"accb{ci}", "lnm"], [f"accb{ci}"])
            dve_op(lambda e, ci=ci: e.tensor_tensor(out=accb[:, ci, :], in0=accb[:, ci, :], in1=lnv[:], op=ALU.mult),
                   [f"accb{ci}", "lnv"], [f"accb{ci}"])
            si = ci % 4
            stg = stage[si]
            j = l * NCC + ci
            act_op(lambda e, ci=ci, stg=stg, j=j: e.activation(out=stg[:], in_=accb[:, ci, :], func=AF.Silu,
                                                              scale=clng[:, j:j + 1], bias=clnb[:, j:j + 1]),
                   [f"accb{ci}", "consts_raw"], [f"stage{si}"])
            r0 = DA + ci * 128
            dstT = cat_s[r0:r0 + 128, t0:t0 + TW]
            P.add("pool", lambda e, dstT=dstT, stg=stg: e.dma_start(out=dstT, in_=stg[:]),
                  reads=[f"stage{si}"], writes=[f"cat_s_{r0 // 128}_{t0 // TW}"], dma_chan=f"stage{si}")

    def phaseB(l):
        if WS.recording:
            return
        barrier_big()
        P.add("sp", lambda e: e.dma_start(out=ft0b[:], in_=Fb_s.rearrange("h b -> (h b)").partition_broadcast(128)),
              reads=[f"Fb_s_{tt}" for tt in range(NT)], writes=["ft0b"], dma_chan="ft0b")
        SB_, OB_, LB_ = (0, 1, 2, 3), (4, 5), (6, 7)
        scount = 0
        for h in range(NH):
            P.add("sp", lambda e, h=h: e.dma_start(out=K_v, in_=kT_s[h * 128:(h + 1) * 128, :]),
                  reads=[f"kT_s_{h}_{tt}" for tt in range(NT)], writes=bigres(0, TOK), dma_chan="Kld")
            vsrc = v_s[:, h * 128:(h + 1) * 128].rearrange("(b p) d -> p b d", p=128)
            VB_ = 16
            for b0 in range(0, NB, VB_):
                b1 = min(NB, b0 + VB_)
                P.add("sp", lambda e, vsrc=vsrc, b0=b0, b1=b1: e.dma_start(out=V_v[:, b0:b1, :], in_=vsrc[:, b0:b1, :]),
                      reads=[f"v_s_{bb // BPT}_{bb % BPT}_{(h * 128) // GC}" for bb in range(b0, b1)],
                      writes=bigres(TOK + b0 * 128, TOK + b1 * 128), dma_chan="Vld")
            for qt in range(NT):
                qb = qbuf[qt % 2]
                qres = f"qbuf{qt % 2}"
                P.add("sp", lambda e, qb=qb, h=h, qt=qt: e.dma_start(out=qb[:], in_=qT_s[h * 128:(h + 1) * 128, qt * TW:(qt + 1) * TW]),
                      reads=[f"qT_s_{h}_{qt}"], writes=[qres], dma_chan=qres)
                b4 = bias4[qt % 2]
                b4res = f"bias4_{qt % 2}"
                nkb = BPT * (qt + 1)
                SQD = float(np.sqrt(128.0))
                cidx = h * NB + qt * BPT
                dve_op(lambda e, b4=b4, h=h, nkb=nkb, cidx=cidx: e.tensor_scalar(
                    out=b4[:, 0, 0:nkb], in0=fcol[:, 0:nkb, h], scalar1=-1.0,
                    scalar2=ft0b[:, cidx:cidx + 1], op0=ALU.mult, op1=ALU.add),
                    ["fcol", "ft0b"], [b4res])
                fr = tmpf[qt % 2]
                frres = f"tmpf{qt % 2}"
                dr = hTb[1][:, qt % 2, :]
                drres = f"hT1_{qt % 2}"
                dm = sq[qt % 2]
                dmres = f"sq{qt % 2}"
                P.add("sp", lambda e, fr=fr, h=h, qt=qt: e.dma_start(out=fr[0:1, :], in_=F_s[h:h + 1, qt * TW:(qt + 1) * TW]),
                      reads=[f"F_s_{qt}"], writes=[frres], dma_chan="frow" + str(qt % 2))
                dve_op(lambda e, fr=fr, dr=dr: e.tensor_scalar(out=dr[0:1, :], in0=fr[0:1, :], scalar1=fr[0:1, 0:1], scalar2=SQD,
                                                               op0=ALU.subtract, op1=ALU.mult),
                       [frres], [drres], small=True)
                bbk = SB_[scount % 4]
                scount += 1
                pe_op(lambda e, bbk=bbk, dr=dr: e.matmul(psum[bbk][:, 0:TW], ones_bf[0:1, :], dr[0:1, :], start=True, stop=True),
                      [drres, "consts"], [f"ps{bbk}"])
                dve_op(lambda e, bbk=bbk, dm=dm: e.tensor_copy(out=dm[:], in_=psum[bbk][:, 0:TW]), [f"ps{bbk}"], [dmres])
                ob, lb = OB_[qt % 2], LB_[qt % 2]
                blk = []
                for kb in range(nkb):
                    blk.append((kb, SB_[scount % 4], pbuf[scount % 4], f"pbuf{scount % 4}"))
                    scount += 1

                def emit_S(kb, sbk, pb, pres, qb=qb, qres=qres, b4=b4, b4res=b4res, qt=qt, dm=dm, dmres=dmres):
                    jmin = max(0, kb - BPT * qt)
                    c0 = jmin * 128
                    diag = kb >= BPT * qt
                    kres = bigres(kb * 128, (kb + 1) * 128)

                    def sfn(e):
                        e.matmul(psum[sbk][:, c0:TW], K_v[:, kb * 128:(kb + 1) * 128], qb[:, c0:TW], start=True, stop=False)
                        ins = e.matmul(psum[sbk][:, c0:TW], ident_bf[:], dm[:, c0:TW], start=False, stop=(not diag))
                        if diag:
                            ins = e.matmul(psum[sbk][:, c0:c0 + 128], ident_bf[:], tri[:], start=False, stop=True)
                        return ins
                    pe_op(sfn, kres + [qres, dmres, "consts"], [f"ps{sbk}"])
                    act_op(lambda e: e.activation(out=pb[:, c0:TW], in_=psum[sbk][:, c0:TW], func=AF.Exp,
                                                  scale=float(INV_SQRT_DH), bias=b4[:, 0, kb:kb + 1]),
                           [f"ps{sbk}", b4res], [pres])

                def emit_PV(kb, sbk, pb, pres, ob=ob, lb=lb, qt=qt, nkb=nkb):
                    jmin = max(0, kb - BPT * qt)
                    c0 = jmin * 128
                    first, last = (kb == 0), (kb == nkb - 1)
                    vres = bigres(TOK + kb * 128, TOK + (kb + 1) * 128)
                    pe_op(lambda e: e.matmul(psum[ob][:, c0:TW], V_v[:, kb, :], pb[:, c0:TW], start=first, stop=last),
                          vres + [pres], [f"ps{ob}"])
                    pe_op(lambda e: e.matmul(psum[lb][:, c0:TW], ones_bf[:], pb[:, c0:TW], start=first, stop=last),
                          [pres, "consts"], [f"ps{lb}"])

                SKEW = 2
                for i in range(nkb + SKEW):
                    if i < nkb:
                        emit_S(*blk[i])
                    if i - SKEW >= 0:
                        emit_PV(*blk[i - SKEW])
                dve_op(lambda e, lb=lb: e.reciprocal(out=rstd_b[:], in_=psum[lb][:, 0:TW]), [f"ps{lb}"], ["rstd_b"])
                si = qt % 4
                stg = stage[si]
                dve_op(lambda e, ob=ob, stg=stg: e.tensor_tensor(out=stg[:], in0=psum[ob][:, 0:TW], in1=rstd_b[:], op=ALU.mult),
                       [f"ps{ob}", "rstd_b"], [f"stage{si}"])
                dstT = cat_s[h * 128:(h + 1) * 128, qt * TW:(qt + 1) * TW]
                P.add("pool", lambda e, dstT=dstT, stg=stg: e.dma_start(out=dstT, in_=stg[:]),
                      reads=[f"stage{si}"], writes=[f"cat_s_{h}_{qt}"], dma_chan=f"stage{si}")
                pump_casts(1)

    def phaseC(l):
        rec = WS.recording
        if not rec:
            pump_casts(10 ** 6)
            dve_op(lambda e: e.memset(fhalo2[0][:], 0.0), [], ["fhalo"], small=True)
            dve_op(lambda e: e.memset(fhalo2[1][:], 0.0), [], ["fhalo"], small=True)
        last_layer = (l == L - 1)
        catres = bigres(cat_off, cat_off + KC * TW)
        for t in range(NT):
            t0 = t * TW
            if not rec:
                src = cat_s[:, t0:t0 + TW].rearrange("(k p) t -> p k t", p=128)
                P.add("sp", lambda e, src=src: e.dma_start(out=cat_v, in_=src),
                      reads=[f"cat_s_{r}_{t}" for r in range(KC)], writes=catres, dma_chan="catld")
                load_x_tile(l, t)
            for gi, (kind, c0, ncols, g0) in enumerate(wout_groups):
                got = WS.get("out", l, gi)
                if rec:
                    continue
                slot_v, slot_res = got
                for oc in range(ncols // 128):
                    co = g0 // 128 + oc
                    bank = next_bank()
                    proj_fm(slot_v, slot_res, oc, KC, cat_v, catres, TW, bank)
                    gm = ada_vec(l, 2)
                    dve_op(lambda e, bank=bank, co=co, gm=gm: e.scalar_tensor_tensor(
                        out=xt[:, co, :], in0=psum[bank][:, 0:TW], scalar=gm[:, co:co + 1], in1=xt[:, co, :],
                        op0=ALU.mult, op1=ALU.add), [f"ps{bank}", f"xt{co}", "ada"], [f"xt{co}"])
            if not rec:
                dbg_dump("dbg_x1", xt[:], [128, KC, TW], F32, [f"xt{k}" for k in range(KC)])
                rms_to_h(gs_f[:, l * KC:(l + 1) * KC], ada_vec(l, 3))
                dbg_dump("dbg_h2", hT[:], [128, KC, TW], BF16, hT_res)
            ngr = len(wup_groups) // 2
            for g in range(ngr):
                gotg = WS.get("up", l, 2 * g)
                gotv = WS.get("up", l, 2 * g + 1)
                if rec:
                    continue
                (sg_v, sg_res), (sv_v, sv_res) = gotg, gotv
                _, _, ncols, g0 = wup_groups[2 * g]
                for oc in range(ncols // 128):
                    ci = g0 // 128 + oc
                    accs = []
                    for which, (s_v, s_res, abuf, aname) in enumerate(((sg_v, sg_res, ag, "ag"), (sv_v, sv_res, av, "av"))):
                        cidx = ci + which * NFC
                        bank = next_bank()
                        proj_fm(s_v, s_res, oc, KC, hT, hT_res, TW, bank)
                        a = abuf[ci % len(abuf)]
                        ares = f"{aname}{ci % len(abuf)}"
                        wb_ = (l * 2 * NFC + cidx) * c.FW
                        bj = l * 2 * NFC + cidx
                        act_op(lambda e, bank=bank, a=a, wb_=wb_, bj=bj: e.activation(
                            out=a[:], in_=psum[bank][:, 0:TW], func=AF.Identity, scale=fdww[:, wb_ + 2:wb_ + 3],
                            bias=fdwb[:, bj:bj + 1]), [f"ps{bank}", "consts_raw"], [ares])

                        def tapfn(e, bank=bank, a=a, wb_=wb_, cidx=cidx, par=t % 2):
                            w1 = fdww[:, wb_ + 1:wb_ + 2]
                            w0 = fdww[:, wb_:wb_ + 1]
                            hnew = fhalo2[1 - par]
                            hold = fhalo2[par]
                            e.tensor_scalar(out=hnew[:, cidx, 0:2], in0=psum[bank][:, TW - 2:TW], scalar1=w0, scalar2=None, op0=ALU.mult)
                            e.scalar_tensor_tensor(out=a[:, 1:TW], in0=psum[bank][:, 0:TW - 1], scalar=w1, in1=a[:, 1:TW],
                                                   op0=ALU.mult, op1=ALU.add)
                            e.scalar_tensor_tensor(out=a[:, 2:TW], in0=psum[bank][:, 0:TW - 2], scalar=w0, in1=a[:, 2:TW],
                                                   op0=ALU.mult, op1=ALU.add)
                            e.scalar_tensor_tensor(out=hnew[:, cidx, 0:1], in0=psum[bank][:, TW - 1:TW], scalar=w1, in1=hnew[:, cidx, 0:1],
                                                   op0=ALU.mult, op1=ALU.add)
                            return e.tensor_tensor(out=a[:, 0:2], in0=a[:, 0:2], in1=hold[:, cidx, 0:2], op=ALU.add)
                        dve_op(tapfn, [f"ps{bank}", ares, "fhalo", "consts_raw"], [ares, "fhalo"])
                        accs.append((a, ares))
                    (a_g, a_gres), (a_v, a_vres) = accs
                    s = tmpf[1 + ci % 2]
                    sres = f"tmpf{1 + ci % 2}"
                    act_op(lambda e, s=s, a_g=a_g: e.activation(out=s[:], in_=a_g[:], func=AF.Silu), [a_gres], [sres])
                    dst = big[:, ci * TW:(ci + 1) * TW]
                    pool_op(lambda e, dst=dst, s=s, a_v=a_v: e.tensor_tensor(out=dst, in0=s[:], in1=a_v[:], op=ALU.mult),
                            [sres, a_vres], bigres(ci * TW, (ci + 1) * TW))
            act_v = big[:, 0:NFC * TW].rearrange("p (k t) -> p k t", t=TW)
            act_res = bigres(0, NFC * TW)
            for gi in range(len(wdn_groups)):
                got = WS.get("dn", l, gi)
                if rec:
                    continue
                slot_v, slot_res = got
                bank = next_bank()
                proj_fm(slot_v, slot_res, 0, NFC, act_v, act_res, TW, bank)
                gf = ada_vec(l, 5)
                dve_op(lambda e, bank=bank, gi=gi, gf=gf: e.scalar_tensor_tensor(
                    out=xt[:, gi, :], in0=psum[bank][:, 0:TW], scalar=gf[:, gi:gi + 1], in1=xt[:, gi, :],
                    op0=ALU.mult, op1=ALU.add), [f"ps{bank}", f"xt{gi}", "ada"], [f"xt{gi}"])
            if rec:
                continue
            xres = [f"xt{k}" for k in range(KC)]
            dbg_dump("dbg_act", big[:, 0:NFC * TW], [128, NFC * TW], BF16, act_res)
            dbg_dump("dbg_x2", xt[:], [128, KC, TW], F32, xres)
            if last_layer:
                rms_to_h(None, None, out_is_final=True, gfin=gfin_sb)
                dst = outT[:, t0:t0 + TW].rearrange("(k p) t -> p k t", p=128)
                P.add("pool", lambda e, dst=dst: e.dma_start(out=dst, in_=xt[:]),
                      reads=xres, writes=["outT"], dma_chan="xst")
            else:
                dst = xs[:, t0:t0 + TW].rearrange("(k p) t -> p k t", p=128)
                P.add("pool", lambda e, dst=dst: e.dma_start(out=dst, in_=xt[:]),
                      reads=xres, writes=[f"xs_{t}"], dma_chan="xst")

    def whole():
        rec = WS.recording
        if not rec:
            prep()
            emit_casts(0)
            pump_casts(10 ** 6)
        for l in range(L):
            if not rec and l + 1 < L:
                emit_casts(l + 1)
            ada_layer(l)
            phaseA(l)
            phaseB(l)
            phaseC(l)

    WS.recording = True
    whole()
    WS.recording = False
    whole()

    chans = sorted(P.chan_count.keys())
    sem_names = list(ENG_NAMES[:4])
    sems = {e: es.enter_context(nc.semaphore(f"s_{e}")) for e in sem_names}
    chan_sems = {ch: es.enter_context(nc.semaphore(f"c_{ch}")) for ch in chans}
    block = es.enter_context(nc.Block())
    run_engine, per_eng = P.emit(nc, sems, chan_sems, None)
    final_waits = [(chan_sems["xst"], P.chan_count["xst"])]

    @block.sync
    def _(e):
        run_engine("sp", e)

    @block.tensor
    def _(e):
        run_engine("pe", e)

    @block.scalar
    def _(e):
        run_engine("act", e)

    @block.vector
    def _(e):
        run_engine("dve", e)

    @block.gpsimd
    def _(e):
        run_engine("pool", e)
        for sem, val in final_waits:
            e.wait_ge(sem, val)
        for ch in chans:
            e.wait_ge(chan_sems[ch], P.chan_count[ch])

    es.close()
    nops = {k: len(v) for k, v in per_eng.items()}
    return nc, nops


def fm(v, nchunk_axis_last=True):
    v = np.asarray(v, dtype=np.float32)
    lead = v.shape[:-1]
    n = v.shape[-1] // 128
    r = v.reshape(lead + (n, 128))
    r = np.moveaxis(r, -1, 0)
    return np.ascontiguousarray(r.reshape(128, -1))


def make_in_maps(cfg, inputs):
    c = cfg
    L = c.L
    common = {
        "ada_w": np.ascontiguousarray(inputs["ada_w"], dtype=np.float32),
        "ada_b": fm(inputs["ada_b"]),
        "g_mix": fm(inputs["mix_norm_g"]),
        "g_ffn": fm(inputs["ffn_norm_g"]),
        "g_fin": fm(inputs["final_norm_g"]),
        "w_in": np.ascontiguousarray(inputs["w_in"], dtype=np.float32),
        "b_fg": np.ascontiguousarray(np.asarray(inputs["b_forget"], dtype=np.float32).T),
        "cdw_w": np.ascontiguousarray(np.transpose(np.asarray(inputs["conf_dw_w"], np.float32).reshape(L, c.CW, c.NCC, 128), (3, 0, 2, 1)).reshape(128, -1)),
        "cdw_b": fm(inputs["conf_dw_b"]),
        "cln_g": fm(inputs["conf_ln_g"]),
        "cln_b": fm(inputs["conf_ln_b"]),
        "sdw_w": np.ascontiguousarray(np.transpose(np.asarray(inputs["sc_dw_w"], np.float32).reshape(L, c.SW, c.NSC, 128), (3, 0, 2, 1)).reshape(128, -1)),
        "w_out": np.ascontiguousarray(inputs["w_out"], dtype=np.float32),
        "w_up": np.ascontiguousarray(inputs["w_up"], dtype=np.float32),
        "fdw_w": np.ascontiguousarray(np.transpose(np.asarray(inputs["ffn_dw_w"], np.float32).reshape(L, c.FW, 2 * c.NFC, 128), (3, 0, 2, 1)).reshape(128, -1)),
        "fdw_b": fm(inputs["ffn_dw_b"]),
        "w_down": np.ascontiguousarray(inputs["w_down"], dtype=np.float32),
        "ident": np.eye(128, dtype=np.float32),
        "tri": np.triu(np.ones((128, 128), dtype=np.float32)),
    }
    x = np.asarray(inputs["x"], dtype=np.float32)
    cc = np.asarray(inputs["c"], dtype=np.float32)
    maps = []
    for b in range(c.ncores):
        m = dict(common)
        m["xT"] = np.ascontiguousarray(x[b].T)
        m["cT"] = fm(cc[b])
        maps.append(m)
    return maps


_CACHE = {}


def run_cfg(cfg, inputs, trace=False):
    key = (cfg.D, cfg.F, cfg.L, cfg.TOK, cfg.ncores)
    if key not in _CACHE:
        _CACHE[key] = build_program(cfg)
    nc, nops = _CACHE[key]
    maps = make_in_maps(cfg, inputs)
    res = run_bass_kernel_spmd(nc, maps, core_ids=list(range(cfg.ncores)), trace=trace)
    B = cfg.ncores
    out = np.stack([np.ascontiguousarray(res.results[b]["outT"].T) for b in range(B)], axis=0)
    return out.astype(np.float32), res


def kernel(**inputs):
    cfg = Cfg()
    out, _ = run_cfg(cfg, inputs)
    return out
```

```python
import numpy as np
import concourse.bass as bass
import concourse.mybir as mybir
from concourse.bass_utils import run_bass_kernel_spmd

F32 = mybir.dt.float32
BF16 = mybir.dt.bfloat16
ALU = mybir.AluOpType
AF = mybir.ActivationFunctionType

ENG_NAMES = ("pe", "act", "dve", "pool", "sp")


class Cfg:
    def __init__(self, D=2048, F=5632, L=4, TOK=8192, TW=512, ncores=4):
        self.D, self.F, self.L, self.TOK, self.TW, self.ncores = D, F, L, TOK, TW, ncores
        self.KC = D // 128
        self.DA = D // 2
        self.NH = self.DA // 128
        self.DC = D // 4
        self.NCC = self.DC // 128
        self.DS = D - self.DA - self.DC
        self.NSC = self.DS // 128
        self.NFC = F // 128
        self.NT = TOK // TW
        self.NB = TOK // 128
        self.BPT = TW // 128
        self.IN_COLS = 3 * self.DA + self.NH + 2 * self.DC + 3 * self.DS
        self.q0, self.k0, self.v0 = 0, self.DA, 2 * self.DA
        self.f0 = 3 * self.DA
        self.cv0 = self.f0 + self.NH
        self.cg0 = self.cv0 + self.DC
        self.sx0 = self.cg0 + self.DC
        self.sb0 = self.sx0 + self.DS
        self.sc0 = self.sb0 + self.DS
        self.CW, self.SW, self.FW = 31, 3, 3


class Res:
    __slots__ = ("name", "writer", "readers")

    def __init__(self, name):
        self.name = name
        self.writer = None
        self.readers = []


class Op:
    __slots__ = ("eng", "fn", "deps", "signal", "sigidx", "chan", "count", "is_dma", "small")

    def __init__(self, eng, fn, is_dma=False, chan=None):
        self.eng, self.fn, self.is_dma, self.chan = eng, fn, is_dma, chan
        self.deps = []
        self.signal = False
        self.sigidx = 0
        self.count = 0


class Prog:
    def __init__(self):
        self.ops = []
        self.chan_count = {}
        self.res = {}
        self.bulk = set()

    def R(self, name):
        r = self.res.get(name)
        if r is None:
            r = self.res[name] = Res(name)
        return r

    def _rl(self, xs):
        out = []
        for x in xs:
            out.append(self.R(x) if isinstance(x, str) else x)
        return out

    def add(self, eng, fn, reads=(), writes=(), dma_chan=None, small=False):
        op = Op(eng, fn, is_dma=dma_chan is not None, chan=dma_chan)
        op.small = small
        deps = set()
        for r in self._rl(reads):
            if r.writer is not None:
                deps.add(r.writer)
            r.readers.append(op)
        for w in self._rl(writes):
            if w.writer is not None and not (dma_chan is not None and w.writer.is_dma and w.writer.chan == dma_chan):
                deps.add(w.writer)
            for rd in w.readers:
                if rd is not op:
                    deps.add(rd)
            w.writer = op
            w.readers = []
        for d in deps:
            if d is op:
                continue
            if (not d.is_dma) and d.eng == eng and not op.is_dma and not d.small:
                continue
            op.deps.append(d)
            d.signal = True
        if dma_chan is not None:
            c = self.chan_count.get(dma_chan, 0) + 16
            self.chan_count[dma_chan] = c
            op.count = c
        self.ops.append(op)
        return op

    def emit(self, nc, sems, chan_sems, engines):
        cnt = {e: 0 for e in ENG_NAMES}
        for op in self.ops:
            if not op.is_dma and op.signal:
                cnt[op.eng] += 1
                op.sigidx = cnt[op.eng]
        per_eng = {e: [] for e in ENG_NAMES}
        for op in self.ops:
            per_eng[op.eng].append(op)

        def run_engine(ename, eobj):
            waited = {}
            for op in per_eng[ename]:
                need = {}
                for d in op.deps:
                    if d.is_dma:
                        key, val = ("c", d.chan), (self.chan_count[d.chan] if d.chan in self.bulk else d.count)
                    else:
                        key, val = ("e", d.eng), d.sigidx
                    if val > need.get(key, 0):
                        need[key] = val
                for key, val in need.items():
                    if waited.get(key, 0) >= val:
                        continue
                    sem = chan_sems[key[1]] if key[0] == "c" else sems[key[1]]
                    eobj.wait_ge(sem, val)
                    waited[key] = val
                ins = op.fn(eobj)
                if op.is_dma:
                    ins.then_inc(chan_sems[op.chan], 16)
                elif op.signal:
                    ins.then_inc(sems[op.eng], 1)

        return run_engine, per_eng


def build_program(cfg, debug_outs=False):
    c = cfg
    D, F, L, TOK, TW, KC, NH, NCC, NSC, NFC, NT, NB, BPT = (
        c.D, c.F, c.L, c.TOK, c.TW, c.KC, c.NH, c.NCC, c.NSC, c.NFC, c.NT, c.NB, c.BPT)
    DA, DC, DS = c.DA, c.DC, c.DS
    nc = bass.Bass("TRN2", target_bir_lowering=False)
    P = Prog()

    def din(name, shape, dt=F32):
        return nc.dram_tensor(name, list(shape), dt, kind="ExternalInput").ap()

    def dscr(name, shape, dt):
        kind = "ExternalOutput" if (debug_outs and not name.startswith("wb_")) else "Internal"
        return nc.dram_tensor(name, list(shape), dt, kind=kind).ap()

    xT = din("xT", [D, TOK])
    cT = din("cT", [128, KC])
    ada_w = din("ada_w", [L, D, 6 * D])
    ada_b = din("ada_b", [128, L * 6 * KC])
    g_mix = din("g_mix", [128, L * KC])
    g_ffn = din("g_ffn", [128, L * KC])
    g_fin = din("g_fin", [128, KC])
    w_in = din("w_in", [L, D, c.IN_COLS])
    b_fg = din("b_fg", [NH, L])
    cdw_w = din("cdw_w", [128, L * NCC * c.CW])
    cdw_b = din("cdw_b", [128, L * NCC])
    cln_g = din("cln_g", [128, L * NCC])
    cln_b = din("cln_b", [128, L * NCC])
    sdw_w = din("sdw_w", [128, L * NSC * c.SW])
    w_out = din("w_out", [L, D, D])
    w_up = din("w_up", [L, D, 2 * F])
    fdw_w = din("fdw_w", [128, L * 2 * NFC * c.FW])
    fdw_b = din("fdw_b", [128, L * 2 * NFC])
    w_down = din("w_down", [L, F, D])
    ident_in = din("ident", [128, 128])
    tri_in = din("tri", [128, 128])
    outT = nc.dram_tensor("outT", [D, TOK], F32, kind="ExternalOutput").ap()

    xs = dscr("xs", [D, TOK], F32)
    qT_s = dscr("qT_s", [DA, TOK], BF16)
    kT_s = dscr("kT_s", [DA, TOK], BF16)
    v_s = dscr("v_s", [TOK, DA], BF16)
    Fb_s = dscr("Fb_s", [NH, NB], F32)
    F_s = dscr("F_s", [NH, TOK], F32)
    cat_s = dscr("cat_s", [D, TOK], BF16)

    GC = 512

    def groups_of(n):
        return [(c0, min(GC, n - c0)) for c0 in range(0, n, GC)]

    win_groups = []
    for nm, c0, n in (("q", c.q0, DA), ("k", c.k0, DA), ("v", c.v0, DA), ("f", c.f0, NH),
                      ("cg", c.cg0, DC), ("cv", c.cv0, DC), ("sx", c.sx0, DS),
                      ("sc", c.sc0, DS), ("sb", c.sb0, DS)):
        for (g0, gn) in groups_of(n):
            win_groups.append((nm, c0 + g0, gn, g0))
    wout_groups = [("o", g0, gn, g0) for (g0, gn) in groups_of(D)]
    wup_groups = []
    for (g0, gn) in groups_of(F):
        wup_groups.append(("ug", g0, gn, g0))
        wup_groups.append(("uv", F + g0, gn, g0))
    wdn_groups = [("d", oc * 128, 128, oc * 128) for oc in range(KC)]
    ada_groups = [("a", g0, gn, g0) for (g0, gn) in groups_of(6 * D)]

    SLOT_ELEMS = max(KC * GC, NFC * 128)
    wb = {}
    for l in range(L):
        wb["in", l] = dscr(f"wb_in{l}", [len(win_groups), 128, KC * GC], BF16)
        wb["out", l] = dscr(f"wb_out{l}", [len(wout_groups), 128, KC * GC], BF16)
        wb["up", l] = dscr(f"wb_up{l}", [len(wup_groups), 128, KC * GC], BF16)
        wb["dn", l] = dscr(f"wb_dn{l}", [len(wdn_groups), 128, NFC * 128], BF16)
        wb["ada", l] = dscr(f"wb_ada{l}", [len(ada_groups), 128, KC * GC], BF16)
    wsrc = {"in": w_in, "out": w_out, "up": w_up, "dn": w_down, "ada": ada_w}
    wgroups = {"in": win_groups, "out": wout_groups, "up": wup_groups, "dn": wdn_groups,
               "ada": ada_groups}
    wkc = {"in": KC, "out": KC, "up": KC, "dn": NFC, "ada": KC}

    import contextlib
    es = contextlib.ExitStack()

    def sb(name, shape, dt=F32):
        return es.enter_context(nc.sbuf_tensor(name, list(shape), dt))

    NSLOT = 3
    ring = [sb(f"ring{i}", [128, SLOT_ELEMS], BF16) for i in range(NSLOT)]
    xt = sb("xt", [128, KC, TW])
    hTb = [sb(f"hT{i}", [128, KC, TW], BF16) for i in range(2)]
    hT = hTb[0]
    sq = [sb(f"sq{i}", [128, TW], BF16) for i in range(2)]
    tmpf = [sb(f"tmpf{i}", [128, TW]) for i in range(3)]
    rstd_b = sb("rstd_b", [128, TW])
    MCS = max(NCC, NSC)
    PA_SIZES = [MCS * TW, MCS * TW, NCC * (c.CW - 1 + TW), NSC * (c.SW - 1 + TW), TW, TW]
    PA_F32 = sum(PA_SIZES)
    BIGN = max(NFC * TW, 2 * TOK, 2 * PA_F32)
    BIGN = (BIGN + 511) // 512 * 512
    big = sb("big", [128, BIGN], BF16)
    bigf = big[:, 0:2 * PA_F32].bitcast(F32)
    _o = [0]

    def carve(n):
        v = bigf[:, _o[0]:_o[0] + n]
        _o[0] += n
        return v
    bufA = carve(PA_SIZES[0]).rearrange("p (c t) -> p c t", t=TW)
    accb = carve(PA_SIZES[1]).rearrange("p (c t) -> p c t", t=TW)
    glu = carve(PA_SIZES[2]).rearrange("p (c t) -> p c t", t=c.CW - 1 + TW)
    tbuf = carve(PA_SIZES[3]).rearrange("p (c t) -> p c t", t=c.SW - 1 + TW)
    lnm = carve(TW)
    lnv = carve(TW)
    PA_NAMES = ([f"bufA{i}" for i in range(MCS)] + [f"accb{i}" for i in range(MCS)] + [f"glu{i}" for i in range(NCC)]
                + [f"tbuf{i}" for i in range(NSC)] + ["glu_h", "tbuf_h", "lnm", "lnv"])
    stage = [sb(f"stage{i}", [128, TW], BF16) for i in range(4)]
    bar_t = sb("bar_t", [128, 2])
    qbuf = [sb(f"qbuf{i}", [128, TW], BF16) for i in range(2)]
    bias4 = [sb(f"bias4_{i}", [128, BPT, NB]) for i in range(2)]
    fsp = tmpf[0][0:NH, :]
    pbuf = [sb(f"pbuf{i}", [128, TW], BF16) for i in range(4)]
    ft = sb("ft", [NH, TW])
    fprev = sb("fprev", [NH, 1])
    fb4 = sb("fb4", [NH, BPT])
    fones = sb("fones", [NH, TW])
    fcol = sb("fcol", [128, NB, NH])
    ft0b = sb("ft0b", [128, NH * NB])
    fhalo2 = [sb(f"fhalo2_{i}", [128, 2 * NFC, 2]) for i in range(2)]
    ag = [sb(f"ag{i}", [128, TW]) for i in range(1)]
    av = [sb(f"av{i}", [128, TW]) for i in range(1)]
    c_sb = sb("c_sb", [128, KC])
    cact = sb("cact", [128, KC], BF16)
    adab_sb = sb("adab_sb", [128, L * 6 * KC])
    ada_sb = sb("ada_sb", [128, L * 6 * KC])
    gmix_sb = sb("gmix_sb", [128, L * KC])
    gffn_sb = sb("gffn_sb", [128, L * KC])
    gfin_sb = sb("gfin_sb", [128, KC])
    gs_m = sb("gs_m", [128, L * KC])
    gs_f = sb("gs_f", [128, L * KC])
    bfg_sb = sb("bfg_sb", [NH, L])
    negb = sb("negb", [NH, L])
    cdww = sb("cdww", [128, L * NCC * c.CW])
    cdwb = sb("cdwb", [128, L * NCC])
    clng = sb("clng", [128, L * NCC])
    clnb = sb("clnb", [128, L * NCC])
    sdww = sb("sdww", [128, L * NSC * c.SW])
    fdww = sb("fdww", [128, L * 2 * NFC * c.FW])
    fdwb = sb("fdwb", [128, L * 2 * NFC])
    ident = sb("ident_sb", [128, 128])
    tri_f = sb("tri_f", [128, 128])
    tri = sb("tri_b", [128, 128], BF16)
    ident_bf = sb("ident_bf", [128, 128], BF16)
    ones_bf = sb("ones_bf", [128, 128], BF16)
    ones_f = sb("ones_f", [128, 128])
    psum = [es.enter_context(nc.psum_tensor(f"ps{i}", [128, 512], F32)) for i in range(8)]

    cat_off = BIGN - KC * TW
    cat_v = big[:, cat_off:cat_off + KC * TW].rearrange("p (k t) -> p k t", t=TW)
    K_v = big[:, 0:TOK]
    V_v = big[:, TOK:2 * TOK].rearrange("p (b d) -> p b d", d=128)

    def bigres(lo, hi):
        return [f"big{i}" for i in range(lo // 512, (hi + 511) // 512)]

    INV_SQRT_DH = 1.0 / np.sqrt(128.0)

    class WStream:
        def __init__(self):
            self.seq = []
            self.recording = True
            self.pos = 0
            self.loaded = 0

        def _issue(self, i):
            kind, l, gi = self.seq[i]
            slot = i % NSLOT
            _, c0, ncols, _ = wgroups[kind][gi]
            kcw = wkc[kind]
            n = kcw * (ncols if kind != "dn" else 128)
            src = wb[kind, l][gi][:, 0:n]
            dst = ring[slot][:, 0:n]
            P.add("sp", lambda e, dst=dst, src=src: e.dma_start(out=dst, in_=src),
                  reads=[f"wb_{kind}{l}_{gi}"], writes=[f"ring{slot}"], dma_chan=f"ring{slot}")

        def get(self, kind, l, gi):
            if self.recording:
                self.seq.append((kind, l, gi))
                return None
            i = self.pos
            assert self.seq[i] == (kind, l, gi)
            self.pos += 1
            while self.loaded < min(len(self.seq), i + NSLOT - 1):
                self._issue(self.loaded)
                self.loaded += 1
            slot = i % NSLOT
            _, c0, ncols, _ = wgroups[kind][gi]
            kcw = wkc[kind]
            nn = ncols if kind != "dn" else 128
            view = ring[slot][:, 0:kcw * nn].rearrange("p (k n) -> p k n", n=nn)
            return view, f"ring{slot}"

    WS = WStream()
    dbg = {}

    def dbg_dump(name, src_ap, shape, dt, reads):
        if not debug_outs or name in dbg or WS.recording:
            return
        t = nc.dram_tensor(name, list(shape), dt, kind="ExternalOutput").ap()
        dbg[name] = t
        P.add("pool", lambda e: e.dma_start(out=t, in_=src_ap), reads=reads, writes=[name], dma_chan="dbg_" + name)

    state = {"mm": 0}

    def next_bank(nb=6):
        b = state["mm"] % nb
        state["mm"] += 1
        return b

    cast_q = []

    def pump_casts(n):
        for _ in range(min(n, len(cast_q))):
            cast_q.pop(0)()

    def emit_casts(l):
        for kind in ("ada", "in", "out", "up", "dn"):
            P.bulk.add(f"cast_{kind}{l}")
            src_all = wsrc[kind][l]
            kcw = wkc[kind]
            for gi, (_, c0, ncols, _) in enumerate(wgroups[kind]):
                src = src_all[:, c0:c0 + ncols].rearrange("(k p) n -> p k n", p=128)
                dst = wb[kind, l][gi][:, 0:kcw * ncols].rearrange("p (k n) -> p k n", n=ncols)
                cast_q.append(lambda dst=dst, src=src, kind=kind, gi=gi: P.add(
                    "pool", lambda e: e.dma_start(out=dst, in_=src, max_dma_last_dim=8192),
                    reads=[], writes=[f"wb_{kind}{l}_{gi}"], dma_chan=f"cast_{kind}{l}"))

    def act_op(fn, reads, writes):
        return P.add("act", fn, reads, writes)

    def dve_op(fn, reads, writes, small=False):
        return P.add("dve", fn, reads, writes, small=small)

    def pool_op(fn, reads, writes):
        return P.add("pool", fn, reads, writes)

    def pe_op(fn, reads, writes):
        return P.add("pe", fn, reads, writes)

    def proj_fm(slot_v, slot_res, oc, kcw, rhs_tile, rhs_res, n, bank):
        def fn(e):
            ins = None
            for k in range(kcw):
                ins = e.matmul(psum[bank][:, 0:n], slot_v[:, k, oc * 128:(oc + 1) * 128],
                               rhs_tile[:, k, 0:n], start=(k == 0), stop=(k == kcw - 1))
            return ins
        pe_op(fn, reads=[slot_res] + rhs_res, writes=[f"ps{bank}"])

    def load_x_tile(l, t):
        src = (xT if l == 0 else xs)[:, t * TW:(t + 1) * TW].rearrange("(k p) t -> p k t", p=128)
        half = KC // 2
        for hf in range(2):
            P.add("sp", lambda e, hf=hf: e.dma_start(out=xt[:, hf * half:(hf + 1) * half, :],
                                                      in_=src[:, hf * half:(hf + 1) * half, :]),
                  reads=([f"xs_{t}"] if l > 0 else []), writes=[f"xt{k}" for k in range(hf * half, (hf + 1) * half)],
                  dma_chan=f"xt{hf}")

    def rms_to_h(gs_ap, sh_ap, out_is_final=False, gfin=None, hb=0):
        hT = hTb[hb]
        SB = 6
        for k in range(KC):
            s = sq[k % 2]
            act_op(lambda e, k=k, s=s: e.activation(out=s[:], in_=xt[:, k, :], func=AF.Square),
                   reads=[f"xt{k}"], writes=[f"sq{k % 2}"])
            pe_op(lambda e, k=k, s=s: e.matmul(psum[SB][:, 0:TW], ones_bf[:], s[:], start=(k == 0), stop=(k == KC - 1)),
                  reads=[f"sq{k % 2}", "consts"], writes=[f"ps{SB}"])
        act_op(lambda e: e.activation(out=rstd_b[:], in_=psum[SB][:, 0:TW], func=AF.Sqrt, scale=1.0 / D, bias=eps_rms[:, 0:1]),
               reads=[f"ps{SB}", "consts"], writes=["rstd_b"])
        dve_op(lambda e: e.reciprocal(out=rstd_b[:], in_=rstd_b[:]), reads=["rstd_b"], writes=["rstd_b"])
        for k in range(KC):
            if out_is_final:
                dve_op(lambda e, k=k: e.scalar_tensor_tensor(out=xt[:, k, :], in0=xt[:, k, :], scalar=gfin[:, k:k + 1],
                                                             in1=rstd_b[:], op0=ALU.mult, op1=ALU.mult),
                       reads=[f"xt{k}", "rstd_b", "consts"], writes=[f"xt{k}"])
            else:
                tf = tmpf[k % 3]
                dve_op(lambda e, k=k, tf=tf: e.tensor_tensor(out=tf[:], in0=xt[:, k, :], in1=rstd_b[:], op=ALU.mult),
                       reads=[f"xt{k}", "rstd_b"], writes=[f"tmpf{k % 3}"])
                act_op(lambda e, k=k, tf=tf: e.activation(out=hT[:, k, :], in_=tf[:], func=AF.Identity,
                                                          scale=gs_ap[:, k:k + 1], bias=sh_ap[:, k:k + 1]),
                       reads=[f"tmpf{k % 3}", "ada"], writes=[f"hT{hb}_{k}"])

    hT_res = [f"hT0_{k}" for k in range(KC)]
    hT_resb = [[f"hT{b}_{k}" for k in range(KC)] for b in range(2)]

    eps_rms = sb("eps_rms", [128, 1])
    eps_ln = sb("eps_ln", [128, 1])
    one_c = sb("one_c", [128, 1])

    def prep():
        loads = [(c_sb, cT), (adab_sb, ada_b), (gmix_sb, g_mix), (gffn_sb, g_ffn), (gfin_sb, g_fin),
                 (bfg_sb, b_fg), (cdww, cdw_w), (cdwb, cdw_b), (clng, cln_g), (clnb, cln_b),
                 (sdww, sdw_w), (fdww, fdw_w), (fdwb, fdw_b), (ident, ident_in), (tri_f, tri_in)]
        for i, (dst, src) in enumerate(loads):
            P.add("sp", lambda e, dst=dst, src=src: e.dma_start(out=dst[:], in_=src),
                  reads=[], writes=["consts_raw"], dma_chan="consts")
        dve_op(lambda e: e.memset(ones_f[:], 1.0), [], ["consts0"], small=True)
        dve_op(lambda e: e.memset(eps_rms[:], 1e-6), [], ["consts0"], small=True)
        dve_op(lambda e: e.memset(eps_ln[:], 1e-5), [], ["consts0"], small=True)
        dve_op(lambda e: e.memset(one_c[:], 1.0), [], ["consts0"], small=True)
        dve_op(lambda e: e.memset(fones[:], 1.0), [], ["consts0"], small=True)
        dve_op(lambda e: e.tensor_copy(out=ones_bf[:], in_=ones_f[:]), ["consts0"], ["consts"], small=True)
        dve_op(lambda e: e.tensor_scalar(out=tri[:], in0=tri_f[:], scalar1=-1.0, scalar2=30000.0, op0=ALU.add, op1=ALU.mult),
               ["consts_raw"], ["consts"], small=True)
        dve_op(lambda e: e.tensor_copy(out=ident_bf[:], in_=ident[:]), ["consts_raw"], ["consts"], small=True)
        dve_op(lambda e: e.tensor_scalar(out=negb[:], in0=bfg_sb[:], scalar1=-1.0, scalar2=None, op0=ALU.mult),
               ["consts_raw"], ["consts"], small=True)
        act_op(lambda e: e.activation(out=cact[:], in_=c_sb[:], func=AF.Silu), ["consts_raw"], ["consts"])

    def ada_layer(l):
        AB = 7
        n6 = 6 * KC
        for gi, (_, c0, ncols, _) in enumerate(ada_groups):
            got = WS.get("ada", l, gi)
            if got is None:
                continue
            slot_v, slot_res = got
            for oc in range(ncols // 128):
                col = c0 // 128 + oc

                def fn(e, oc=oc, col=col, slot_v=slot_v):
                    ins = None
                    for k in range(KC):
                        ins = e.matmul(psum[AB][:, col:col + 1], slot_v[:, k, oc * 128:(oc + 1) * 128],
                                       cact[:, k:k + 1], start=(k == 0), stop=(k == KC - 1))
                    return ins
                pe_op(fn, reads=[slot_res, "consts"], writes=[f"ps{AB}"])
        if WS.recording:
            return
        dve_op(lambda e: e.tensor_tensor(out=ada_sb[:, l * n6:(l + 1) * n6], in0=psum[AB][:, 0:n6],
                                         in1=adab_sb[:, l * n6:(l + 1) * n6], op=ALU.add),
               [f"ps{AB}", "consts_raw"], ["ada"], small=True)
        a0 = l * n6
        dve_op(lambda e: e.scalar_tensor_tensor(out=gs_m[:, l * KC:(l + 1) * KC], in0=ada_sb[:, a0 + KC:a0 + 2 * KC],
                                                scalar=1.0, in1=gmix_sb[:, l * KC:(l + 1) * KC], op0=ALU.add, op1=ALU.mult),
               ["ada", "consts_raw"], ["ada"], small=True)
        dve_op(lambda e: e.scalar_tensor_tensor(out=gs_f[:, l * KC:(l + 1) * KC], in0=ada_sb[:, a0 + 4 * KC:a0 + 5 * KC],
                                                scalar=1.0, in1=gffn_sb[:, l * KC:(l + 1) * KC], op0=ALU.add, op1=ALU.mult),
               ["ada", "consts_raw"], ["ada"], small=True)

    def ada_vec(l, which):
        dbg_dump("dbg_ada", ada_sb[:], [128, L * 6 * KC], F32, ["ada"])
        dbg_dump("dbg_cact", cact[:], [128, KC], BF16, ["consts"])
        a0 = l * 6 * KC + which * KC
        return ada_sb[:, a0:a0 + KC]

    def barrier_big():
        dve_op(lambda e: e.memset(bar_t[:], 0.0), [], PA_NAMES + bigres(0, BIGN), small=True)

    def phaseA(l):
        rec = WS.recording
        if not rec:
            barrier_big()
            for ci in range(NCC):
                dve_op(lambda e, ci=ci: e.memset(glu[:, ci, 0:c.CW - 1], 0.0), [], ["glu_h"], small=True)
            for ci in range(NSC):
                dve_op(lambda e, ci=ci: e.memset(tbuf[:, ci, 0:c.SW - 1], 0.0), [], ["tbuf_h"], small=True)
            dve_op(lambda e: e.memset(fprev[:], 0.0), [], ["fprev"], small=True)
            load_x_tile(l, 0)
        sstate = {"st": 0, "vs": 0}
        NORM_AT = min(3, len(win_groups) - 1)
        if not rec:
            rms_to_h(gs_m[:, l * KC:(l + 1) * KC], ada_vec(l, 0), hb=0)
            if NT > 1:
                load_x_tile(l, 1)
        for t in range(NT):
            t0 = t * TW
            hT = hTb[t % 2]
            hT_res = hT_resb[t % 2]
            for gi, (kind, c0, ncols, g0) in enumerate(win_groups):
                got = WS.get("in", l, gi)
                if rec:
                    continue
                if gi == NORM_AT and t + 1 < NT:
                    rms_to_h(gs_m[:, l * KC:(l + 1) * KC], ada_vec(l, 0), hb=(t + 1) % 2)
                    if t + 2 < NT:
                        load_x_tile(l, t + 2)
                slot_v, slot_res = got
                if kind == "v":
                    for tb in range(BPT):
                        bank = next_bank()

                        def fn(e, tb=tb, bank=bank, slot_v=slot_v, ncols=ncols, hT=hT):
                            ins = None
                            for k in range(KC):
                                ins = e.matmul(psum[bank][:, 0:ncols], hT[:, k, tb * 128:(tb + 1) * 128],
                                               slot_v[:, k, 0:ncols], start=(k == 0), stop=(k == KC - 1))
                            return ins
                        pe_op(fn, reads=[slot_res] + hT_res, writes=[f"ps{bank}"])
                        vi = sstate["st"] % 4
                        sstate["st"] += 1
                        vsb = stage[vi]
                        act_op(lambda e, bank=bank, vsb=vsb, ncols=ncols: e.activation(out=vsb[:, 0:ncols], in_=psum[bank][:, 0:ncols], func=AF.Copy),
                               [f"ps{bank}"], [f"stage{vi}"])
                        dst = v_s[t0 + tb * 128:t0 + (tb + 1) * 128, g0:g0 + ncols]
                        P.add("pool", lambda e, dst=dst, vsb=vsb, ncols=ncols: e.dma_start(out=dst, in_=vsb[:, 0:ncols]),
                              reads=[f"stage{vi}"], writes=[f"v_s_{t}_{tb}_{g0 // GC}"], dma_chan=f"stage{vi}")
                    continue
                if kind == "f":
                    bank = next_bank()

                    def fn(e, bank=bank, slot_v=slot_v, hT=hT):
                        ins = None
                        for k in range(KC):
                            ins = e.matmul(psum[bank][0:NH, 0:TW], slot_v[:, k, 0:NH], hT[:, k, :],
                                           start=(k == 0), stop=(k == KC - 1))
                        return ins
                    pe_op(fn, reads=[slot_res] + hT_res, writes=[f"ps{bank}"])
                    act_op(lambda e, bank=bank: e.activation(out=fsp[:], in_=psum[bank][0:NH, 0:TW], func=AF.Exp,
                                                             scale=-1.0, bias=negb[:, l:l + 1]),
                           [f"ps{bank}", "consts"], ["tmpf0"])
                    act_op(lambda e: e.activation(out=fsp[:], in_=fsp[:], func=AF.Ln, scale=1.0, bias=one_c[0:NH, 0:1]),
                           ["tmpf0", "consts0"], ["tmpf0"])
                    dve_op(lambda e: e.tensor_tensor_scan(out=ft[:], data0=fones[:], data1=fsp[:], initial=fprev[:, 0:1],
                                                          op0=ALU.mult, op1=ALU.subtract),
                           ["tmpf0", "fprev", "consts0"], ["ft"], small=True)
                    dve_op(lambda e: e.tensor_copy(out=fprev[:], in_=ft[:, TW - 1:TW]), ["ft"], ["fprev"], small=True)
                    P.add("pool", lambda e, t0=t0: e.dma_start(out=F_s[:, t0:t0 + TW], in_=ft[:]),
                          reads=["ft"], writes=[f"F_s_{t}"], dma_chan="frow_st")
                    src = ft[:, 0:TW].rearrange("h (b p) -> h b p", p=128)[:, :, 0]
                    dve_op(lambda e, src=src: e.tensor_copy(out=fb4[:], in_=src), ["ft"], ["fb4"], small=True)
                    P.add("pool", lambda e, t=t: e.dma_start(out=Fb_s[:, t * BPT:(t + 1) * BPT], in_=fb4[:]),
                          reads=["fb4"], writes=[f"Fb_s_{t}"], dma_chan="fb")
                    for bb in range(BPT):
                        tbank = next_bank()
                        pe_op(lambda e, bb=bb, tbank=tbank: e.transpose(out=psum[tbank][:, 0:NH], in_=ft[0:NH, bb * 128:(bb + 1) * 128],
                                                                         identity=ident[0:NH, 0:NH]),
                              ["ft", "consts_raw"], [f"ps{tbank}"])
                        kb = t * BPT + bb
                        dve_op(lambda e, tbank=tbank, kb=kb: e.tensor_copy(out=fcol[:, kb, :], in_=psum[tbank][:, 0:NH]),
                               [f"ps{tbank}"], ["fcol"])
                    continue
                for oc in range(ncols // 128):
                    ci = (g0 // 128) + oc
                    bank = next_bank()
                    proj_fm(slot_v, slot_res, oc, KC, hT, hT_res, TW, bank)
                    pr = f"ps{bank}"
                    if kind in ("q", "k"):
                        si = sstate["st"] % 4
                        sstate["st"] += 1
                        stg = stage[si]
                        if kind == "q":
                            act_op(lambda e, bank=bank, stg=stg: e.activation(out=stg[:], in_=psum[bank][:, 0:TW], func=AF.Copy),
                                   [pr], [f"stage{si}"])
                        else:
                            dve_op(lambda e, bank=bank, stg=stg: e.tensor_copy(out=stg[:], in_=psum[bank][:, 0:TW]),
                                   [pr], [f"stage{si}"])
                        dstT = (qT_s if kind == "q" else kT_s)[ci * 128:(ci + 1) * 128, t0:t0 + TW]
                        P.add("pool", lambda e, dstT=dstT, stg=stg: e.dma_start(out=dstT, in_=stg[:]),
                              reads=[f"stage{si}"], writes=[("qT_s" if kind == "q" else "kT_s") + f"_{ci}_{t}"], dma_chan=f"stage{si}")
                    elif kind == "cg":
                        act_op(lambda e, bank=bank, ci=ci: e.activation(out=bufA[:, ci, :], in_=psum[bank][:, 0:TW], func=AF.Sigmoid),
                               [pr], [f"bufA{ci}"])
                    elif kind == "cv":
                        H = c.CW - 1
                        dve_op(lambda e, bank=bank, ci=ci, H=H: e.tensor_tensor(out=glu[:, ci, H:H + TW], in0=psum[bank][:, 0:TW],
                                                                           in1=bufA[:, ci, :], op=ALU.mult),
                               [pr, f"bufA{ci}"], [f"glu{ci}"])
                        wbase = (l * NCC + ci) * c.CW

                        def convfn(e, ci=ci, wbase=wbase, H=H):
                            ins = e.tensor_scalar(out=accb[:, ci, :], in0=glu[:, ci, H:H + TW],
                                                  scalar1=cdww[:, wbase + H:wbase + H + 1],
                                                  scalar2=cdwb[:, l * NCC + ci:l * NCC + ci + 1], op0=ALU.mult, op1=ALU.add)
                            for kk in range(H):
                                ins = e.scalar_tensor_tensor(out=accb[:, ci, :], in0=glu[:, ci, kk:kk + TW],
                                                             scalar=cdww[:, wbase + kk:wbase + kk + 1], in1=accb[:, ci, :],
                                                             op0=ALU.mult, op1=ALU.add)
                            return ins
                        dve_op(convfn, [f"glu{ci}", "glu_h", "consts_raw"], [f"accb{ci}"])
                        act_op(lambda e, ci=ci, H=H: e.activation(out=glu[:, ci, 0:H], in_=glu[:, ci, TW:TW + H], func=AF.Copy),
                               [f"glu{ci}", f"accb{ci}"], ["glu_h"])
                        if ci == NCC - 1:
                            conf_ln_out(l, t0)
                    elif kind == "sx":
                        act_op(lambda e, bank=bank, ci=ci: e.activation(out=bufA[:, ci, :], in_=psum[bank][:, 0:TW], func=AF.Copy),
                               [pr], [f"bufA{ci}"])
                    elif kind == "sc":
                        H = c.SW - 1
                        dve_op(lambda e, bank=bank, ci=ci, H=H: e.tensor_tensor(out=tbuf[:, ci, H:H + TW], in0=psum[bank][:, 0:TW],
                                                                                 in1=bufA[:, ci, :], op=ALU.mult),
                               [pr, f"bufA{ci}"], [f"tbuf{ci}"])
                        wbase = (l * NSC + ci) * c.SW

                        def sconvfn(e, ci=ci, wbase=wbase, H=H):
                            ins = e.tensor_scalar(out=accb[:, ci, :], in0=tbuf[:, ci, H:H + TW],
                                                  scalar1=sdww[:, wbase + H:wbase + H + 1], scalar2=None, op0=ALU.mult)
                            for kk in range(H):
                                ins = e.scalar_tensor_tensor(out=accb[:, ci, :], in0=tbuf[:, ci, kk:kk + TW],
                                                             scalar=sdww[:, wbase + kk:wbase + kk + 1], in1=accb[:, ci, :],
                                                             op0=ALU.mult, op1=ALU.add)
                            return ins
                        dve_op(sconvfn, [f"tbuf{ci}", "tbuf_h", "consts_raw"], [f"accb{ci}"])
                        act_op(lambda e, ci=ci, H=H: e.activation(out=tbuf[:, ci, 0:H], in_=tbuf[:, ci, TW:TW + H], func=AF.Copy),
                               [f"tbuf{ci}", f"accb{ci}"], ["tbuf_h"])
                    elif kind == "sb":
                        si = sstate["st"] % 4
                        sstate["st"] += 1
                        stg = stage[si]
                        dve_op(lambda e, bank=bank, ci=ci, stg=stg: e.tensor_tensor(out=stg[:], in0=psum[bank][:, 0:TW],
                                                                                    in1=accb[:, ci, :], op=ALU.mult),
                               [pr, f"accb{ci}"], [f"stage{si}"])
                        r0 = DA + DC + ci * 128
                        dstT = cat_s[r0:r0 + 128, t0:t0 + TW]
                        P.add("pool", lambda e, dstT=dstT, stg=stg: e.dma_start(out=dstT, in_=stg[:]),
                              reads=[f"stage{si}"], writes=[f"cat_s_{r0 // 128}_{t}"], dma_chan=f"stage{si}")

        def _unused():
            pass

    def conf_ln_out(l, t0):
        MB, VB = 6, 7
        for ci in range(NCC):
            pe_op(lambda e, ci=ci: e.matmul(psum[MB][:, 0:TW], ones_f[:], accb[:, ci, :], start=(ci == 0), stop=(ci == NCC - 1)),
                  [f"accb{ci}", "consts0"], [f"ps{MB}"])
        for ci in range(NCC):
            tf = tmpf[ci % 3]
            act_op(lambda e, ci=ci, tf=tf: e.activation(out=tf[:], in_=accb[:, ci, :], func=AF.Square),
                   [f"accb{ci}"], [f"tmpf{ci % 3}"])
            pe_op(lambda e, ci=ci, tf=tf: e.matmul(psum[VB][:, 0:TW], ones_f[:], tf[:], start=(ci == 0), stop=(ci == NCC - 1)),
                  [f"tmpf{ci % 3}", "consts0"], [f"ps{VB}"])
        dbg_dump("dbg_accb", accb[:], [128, max(NCC, NSC), TW], F32, [f"accb{ci}" for ci in range(NCC)])
        dbg_dump("dbg_glu", glu[:], [128, NCC, c.CW - 1 + TW], F32, [f"glu{ci}" for ci in range(NCC)])
        act_op(lambda e: e.activation(out=lnm[:], in_=psum[MB][:, 0:TW], func=AF.Copy, scale=1.0 / DC),
               [f"ps{MB}"], ["lnm"])
        dbg_dump("dbg_lnm", lnm[:], [128, TW], F32, ["lnm"])
        dve_op(lambda e: e.tensor_tensor(out=lnv[:], in0=lnm[:], in1=lnm[:], op=ALU.mult), ["lnm"], ["lnv"])
        dve_op(lambda e: e.scalar_tensor_tensor(out=lnv[:], in0=psum[VB][:, 0:TW], scalar=1.0 / DC, in1=lnv[:],
                                                op0=ALU.mult, op1=ALU.subtract),
               [f"ps{VB}", "lnv"], ["lnv"])
        act_op(lambda e: e.activation(out=lnv[:], in_=lnv[:], func=AF.Sqrt, scale=1.0, bias=eps_ln[:, 0:1]),
               ["lnv", "consts0"], ["lnv"])
        dve_op(lambda e: e.reciprocal(out=lnv[:], in_=lnv[:]), ["lnv"], ["lnv"])
        dbg_dump("dbg_lnv", lnv[:], [128, TW], F32, ["lnv"])
        for ci in range(NCC):
            dve_op(lambda e, ci=ci: e.tensor_tensor(out=accb[:, ci, :], in0=accb[:, ci, :], in1=lnm[:], op=ALU.subtract),
                   [f"accb{ci}", "lnm"], [f"accb{ci}"])
            dve_op(lambda e, ci=ci: e.tensor_tensor(out=accb[:, ci, :], in0=accb[:, ci, :], in1=lnv[:], op=ALU.mult),
                   [f"accb{ci}", "lnv"], [f"accb{ci}"])
            si = ci % 4
            stg = stage[si]
            j = l * NCC + ci
            act_op(lambda e, ci=ci, stg=stg, j=j: e.activation(out=stg[:], in_=accb[:, ci, :], func=AF.Silu,
                                                              scale=clng[:, j:j + 1], bias=clnb[:, j:j + 1]),
                   [f"accb{ci}", "consts_raw"], [f"stage{si}"])
            r0 = DA + ci * 128
            dstT = cat_s[r0:r0 + 128, t0:t0 + TW]
            P.add("pool", lambda e, dstT=dstT, stg=stg: e.dma_start(out=dstT, in_=stg[:]),
                  reads=[f"stage{si}"], writes=[f"cat_s_{r0 // 128}_{t0 // TW}"], dma_chan=f"stage{si}")

    def phaseB(l):
        if WS.recording:
            return
        barrier_big()
        P.add("sp", lambda e: e.dma_start(out=ft0b[:], in_=Fb_s.rearrange("h b -> (h b)").partition_broadcast(128)),
              reads=[f"Fb_s_{tt}" for tt in range(NT)], writes=["ft0b"], dma_chan="ft0b")
        SB_, OB_, LB_ = (0, 1, 2, 3), (4, 5), (6, 7)
        scount = 0
        for h in range(NH):
            P.add("sp", lambda e, h=h: e.dma_start(out=K_v, in_=kT_s[h * 128:(h + 1) * 128, :]),
                  reads=[f"kT_s_{h}_{tt}" for tt in range(NT)], writes=bigres(0, TOK), dma_chan="Kld")
            vsrc = v_s[:, h * 128:(h + 1) * 128].rearrange("(b p) d -> p b d", p=128)
            VB_ = 16
            for b0 in range(0, NB, VB_):
                b1 = min(NB, b0 + VB_)
                P.add("sp", lambda e, vsrc=vsrc, b0=b0, b1=b1: e.dma_start(out=V_v[:, b0:b1, :], in_=vsrc[:, b0:b1, :]),
                      reads=[f"v_s_{bb // BPT}_{bb % BPT}_{(h * 128) // GC}" for bb in range(b0, b1)],
                      writes=bigres(TOK + b0 * 128, TOK + b1 * 128), dma_chan="Vld")
            for qt in range(NT):
                qb = qbuf[qt % 2]
                qres = f"qbuf{qt % 2}"
                P.add("sp", lambda e, qb=qb, h=h, qt=qt: e.dma_start(out=qb[:], in_=qT_s[h * 128:(h + 1) * 128, qt * TW:(qt + 1) * TW]),
                      reads=[f"qT_s_{h}_{qt}"], writes=[qres], dma_chan=qres)
                b4 = bias4[qt % 2]
                b4res = f"bias4_{qt % 2}"
                nkb = BPT * (qt + 1)
                SQD = float(np.sqrt(128.0))
                cidx = h * NB + qt * BPT
                dve_op(lambda e, b4=b4, h=h, nkb=nkb, cidx=cidx: e.tensor_scalar(
                    out=b4[:, 0, 0:nkb], in0=fcol[:, 0:nkb, h], scalar1=-1.0,
                    scalar2=ft0b[:, cidx:cidx + 1], op0=ALU.mult, op1=ALU.add),
                    ["fcol", "ft0b"], [b4res])
                fr = tmpf[qt % 2]
                frres = f"tmpf{qt % 2}"
                dr = hTb[1][:, qt % 2, :]
                drres = f"hT1_{qt % 2}"
                dm = sq[qt % 2]
                dmres = f"sq{qt % 2}"
                P.add("sp", lambda e, fr=fr, h=h, qt=qt: e.dma_start(out=fr[0:1, :], in_=F_s[h:h + 1, qt * TW:(qt + 1) * TW]),
                      reads=[f"F_s_{qt}"], writes=[frres], dma_chan="frow" + str(qt % 2))
                dve_op(lambda e, fr=fr, dr=dr: e.tensor_scalar(out=dr[0:1, :], in0=fr[0:1, :], scalar1=fr[0:1, 0:1], scalar2=SQD,
                                                               op0=ALU.subtract, op1=ALU.mult),
                       [frres], [drres], small=True)
                bbk = SB_[scount % 4]
                scount += 1
                pe_op(lambda e, bbk=bbk, dr=dr: e.matmul(psum[bbk][:, 0:TW], ones_bf[0:1, :], dr[0:1, :], start=True, stop=True),
                      [drres, "consts"], [f"ps{bbk}"])
                dve_op(lambda e, bbk=bbk, dm=dm: e.tensor_copy(out=dm[:], in_=psum[bbk][:, 0:TW]), [f"ps{bbk}"], [dmres])
                ob, lb = OB_[qt % 2], LB_[qt % 2]
                blk = []
                for kb in range(nkb):
                    blk.append((kb, SB_[scount % 4], pbuf[scount % 4], f"pbuf{scount % 4}"))
                    scount += 1

                def emit_S(kb, sbk, pb, pres, qb=qb, qres=qres, b4=b4, b4res=b4res, qt=qt, dm=dm, dmres=dmres):
                    jmin = max(0, kb - BPT * qt)
                    c0 = jmin * 128
                    diag = kb >= BPT * qt
                    kres = bigres(kb * 128, (kb + 1) * 128)

                    def sfn(e):
                        ins = e.matmul(psum[sbk][:, c0:TW], K_v[:, kb * 128:(kb + 1) * 128], qb[:, c0:TW], start=True, stop=(not diag))
                        if diag:
                            ins = e.matmul(psum[sbk][:, c0:c0 + 128], ident_bf[:], tri[:], start=False, stop=True)
                        return ins
                    pe_op(sfn, kres + [qres, "consts"], [f"ps{sbk}"])
                    dve_op(lambda e: e.tensor_tensor(out=psum[sbk][:, c0:TW], in0=psum[sbk][:, c0:TW], in1=dm[:, c0:TW], op=ALU.add),
                           [f"ps{sbk}", dmres], [f"ps{sbk}"])
                    act_op(lambda e: e.activation(out=pb[:, c0:TW], in_=psum[sbk][:, c0:TW], func=AF.Exp,
                                                  scale=float(INV_SQRT_DH), bias=b4[:, 0, kb:kb + 1]),
                           [f"ps{sbk}", b4res], [pres])

                def emit_PV(kb, sbk, pb, pres, ob=ob, lb=lb, qt=qt, nkb=nkb):
                    jmin = max(0, kb - BPT * qt)
                    c0 = jmin * 128
                    first, last = (kb == 0), (kb == nkb - 1)
                    vres = bigres(TOK + kb * 128, TOK + (kb + 1) * 128)
                    pe_op(lambda e: e.matmul(psum[ob][:, c0:TW], V_v[:, kb, :], pb[:, c0:TW], start=first, stop=last),
                          vres + [pres], [f"ps{ob}"])
                    pe_op(lambda e: e.matmul(psum[lb][:, c0:TW], ones_bf[:], pb[:, c0:TW], start=first, stop=last),
                          [pres, "consts"], [f"ps{lb}"])

                SKEW = 2
                for i in range(nkb + SKEW):
                    if i < nkb:
                        emit_S(*blk[i])
                    if i - SKEW >= 0:
                        emit_PV(*blk[i - SKEW])
                dve_op(lambda e, lb=lb: e.reciprocal(out=rstd_b[:], in_=psum[lb][:, 0:TW]), [f"ps{lb}"], ["rstd_b"])
                si = qt % 4
                stg = stage[si]
                dve_op(lambda e, ob=ob, stg=stg: e.tensor_tensor(out=stg[:], in0=psum[ob][:, 0:TW], in1=rstd_b[:], op=ALU.mult),
                       [f"ps{ob}", "rstd_b"], [f"stage{si}"])
                dstT = cat_s[h * 128:(h + 1) * 128, qt * TW:(qt + 1) * TW]
                P.add("pool", lambda e, dstT=dstT, stg=stg: e.dma_start(out=dstT, in_=stg[:]),
                      reads=[f"stage{si}"], writes=[f"cat_s_{h}_{qt}"], dma_chan=f"stage{si}")
                pump_casts(1)

    def phaseC(l):
        rec = WS.recording
        if not rec:
            pump_casts(10 ** 6)
            dve_op(lambda e: e.memset(fhalo2[0][:], 0.0), [], ["fhalo"], small=True)
            dve_op(lambda e: e.memset(fhalo2[1][:], 0.0), [], ["fhalo"], small=True)
        last_layer = (l == L - 1)
        catres = bigres(cat_off, cat_off + KC * TW)
        for t in range(NT):
            t0 = t * TW
            if not rec:
                src = cat_s[:, t0:t0 + TW].rearrange("(k p) t -> p k t", p=128)
                P.add("sp", lambda e, src=src: e.dma_start(out=cat_v, in_=src),
                      reads=[f"cat_s_{r}_{t}" for r in range(KC)], writes=catres, dma_chan="catld")
                load_x_tile(l, t)
            for gi, (kind, c0, ncols, g0) in enumerate(wout_groups):
                got = WS.get("out", l, gi)
                if rec:
                    continue
                slot_v, slot_res = got
                for oc in range(ncols // 128):
                    co = g0 // 128 + oc
                    bank = next_bank()
                    proj_fm(slot_v, slot_res, oc, KC, cat_v, catres, TW, bank)
                    gm = ada_vec(l, 2)
                    dve_op(lambda e, bank=bank, co=co, gm=gm: e.scalar_tensor_tensor(
                        out=xt[:, co, :], in0=psum[bank][:, 0:TW], scalar=gm[:, co:co + 1], in1=xt[:, co, :],
                        op0=ALU.mult, op1=ALU.add), [f"ps{bank}", f"xt{co}", "ada"], [f"xt{co}"])
            if not rec:
                dbg_dump("dbg_x1", xt[:], [128, KC, TW], F32, [f"xt{k}" for k in range(KC)])
                rms_to_h(gs_f[:, l * KC:(l + 1) * KC], ada_vec(l, 3))
                dbg_dump("dbg_h2", hT[:], [128, KC, TW], BF16, hT_res)
            ngr = len(wup_groups) // 2
            for g in range(ngr):
                gotg = WS.get("up", l, 2 * g)
                gotv = WS.get("up", l, 2 * g + 1)
                if rec:
                    continue
                (sg_v, sg_res), (sv_v, sv_res) = gotg, gotv
                _, _, ncols, g0 = wup_groups[2 * g]
                for oc in range(ncols // 128):
                    ci = g0 // 128 + oc
                    accs = []
                    for which, (s_v, s_res, abuf, aname) in enumerate(((sg_v, sg_res, ag, "ag"), (sv_v, sv_res, av, "av"))):
                        cidx = ci + which * NFC
                        bank = next_bank()
                        proj_fm(s_v, s_res, oc, KC, hT, hT_res, TW, bank)
                        a = abuf[ci % len(abuf)]
                        ares = f"{aname}{ci % len(abuf)}"
                        wb_ = (l * 2 * NFC + cidx) * c.FW
                        bj = l * 2 * NFC + cidx
                        act_op(lambda e, bank=bank, a=a, wb_=wb_, bj=bj: e.activation(
                            out=a[:], in_=psum[bank][:, 0:TW], func=AF.Identity, scale=fdww[:, wb_ + 2:wb_ + 3],
                            bias=fdwb[:, bj:bj + 1]), [f"ps{bank}", "consts_raw"], [ares])

                        def tapfn(e, bank=bank, a=a, wb_=wb_, cidx=cidx, par=t % 2):
                            w1 = fdww[:, wb_ + 1:wb_ + 2]
                            w0 = fdww[:, wb_:wb_ + 1]
                            hnew = fhalo2[1 - par]
                            hold = fhalo2[par]
                            e.tensor_scalar(out=hnew[:, cidx, 0:2], in0=psum[bank][:, TW - 2:TW], scalar1=w0, scalar2=None, op0=ALU.mult)
                            e.scalar_tensor_tensor(out=a[:, 1:TW], in0=psum[bank][:, 0:TW - 1], scalar=w1, in1=a[:, 1:TW],
                                                   op0=ALU.mult, op1=ALU.add)
                            e.scalar_tensor_tensor(out=a[:, 2:TW], in0=psum[bank][:, 0:TW - 2], scalar=w0, in1=a[:, 2:TW],
                                                   op0=ALU.mult, op1=ALU.add)
                            e.scalar_tensor_tensor(out=hnew[:, cidx, 0:1], in0=psum[bank][:, TW - 1:TW], scalar=w1, in1=hnew[:, cidx, 0:1],
                                                   op0=ALU.mult, op1=ALU.add)
                            return e.tensor_tensor(out=a[:, 0:2], in0=a[:, 0:2], in1=hold[:, cidx, 0:2], op=ALU.add)
                        dve_op(tapfn, [f"ps{bank}", ares, "fhalo", "consts_raw"], [ares, "fhalo"])
                        accs.append((a, ares))
                    (a_g, a_gres), (a_v, a_vres) = accs
                    s = tmpf[1 + ci % 2]
                    sres = f"tmpf{1 + ci % 2}"
                    act_op(lambda e, s=s, a_g=a_g: e.activation(out=s[:], in_=a_g[:], func=AF.Silu), [a_gres], [sres])
                    dst = big[:, ci * TW:(ci + 1) * TW]
                    pool_op(lambda e, dst=dst, s=s, a_v=a_v: e.tensor_tensor(out=dst, in0=s[:], in1=a_v[:], op=ALU.mult),
                            [sres, a_vres], bigres(ci * TW, (ci + 1) * TW))
            act_v = big[:, 0:NFC * TW].rearrange("p (k t) -> p k t", t=TW)
            act_res = bigres(0, NFC * TW)
            for gi in range(len(wdn_groups)):
                got = WS.get("dn", l, gi)
                if rec:
                    continue
                slot_v, slot_res = got
                bank = next_bank()
                proj_fm(slot_v, slot_res, 0, NFC, act_v, act_res, TW, bank)
                gf = ada_vec(l, 5)
                dve_op(lambda e, bank=bank, gi=gi, gf=gf: e.scalar_tensor_tensor(
                    out=xt[:, gi, :], in0=psum[bank][:, 0:TW], scalar=gf[:, gi:gi + 1], in1=xt[:, gi, :],
                    op0=ALU.mult, op1=ALU.add), [f"ps{bank}", f"xt{gi}", "ada"], [f"xt{gi}"])
            if rec:
                continue
            xres = [f"xt{k}" for k in range(KC)]
            dbg_dump("dbg_act", big[:, 0:NFC * TW], [128, NFC * TW], BF16, act_res)
            dbg_dump("dbg_x2", xt[:], [128, KC, TW], F32, xres)
            if last_layer:
                rms_to_h(None, None, out_is_final=True, gfin=gfin_sb)
                dst = outT[:, t0:t0 + TW].rearrange("(k p) t -> p k t", p=128)
                P.add("pool", lambda e, dst=dst: e.dma_start(out=dst, in_=xt[:]),
                      reads=xres, writes=["outT"], dma_chan="xst")
            else:
                dst = xs[:, t0:t0 + TW].rearrange("(k p) t -> p k t", p=128)
                P.add("pool", lambda e, dst=dst: e.dma_start(out=dst, in_=xt[:]),
                      reads=xres, writes=[f"xs_{t}"], dma_chan="xst")

    def whole():
        rec = WS.recording
        if not rec:
            prep()
            emit_casts(0)
            pump_casts(10 ** 6)
        for l in range(L):
            if not rec and l + 1 < L:
                emit_casts(l + 1)
            ada_layer(l)
            phaseA(l)
            phaseB(l)
            phaseC(l)

    WS.recording = True
    whole()
    WS.recording = False
    whole()

    chans = sorted(P.chan_count.keys())
    sem_names = list(ENG_NAMES[:4])
    sems = {e: es.enter_context(nc.semaphore(f"s_{e}")) for e in sem_names}
    chan_sems = {ch: es.enter_context(nc.semaphore(f"c_{ch}")) for ch in chans}
    block = es.enter_context(nc.Block())
    run_engine, per_eng = P.emit(nc, sems, chan_sems, None)
    final_waits = [(chan_sems["xst"], P.chan_count["xst"])]

    @block.sync
    def _(e):
        run_engine("sp", e)

    @block.tensor
    def _(e):
        run_engine("pe", e)

    @block.scalar
    def _(e):
        run_engine("act", e)

    @block.vector
    def _(e):
        run_engine("dve", e)

    @block.gpsimd
    def _(e):
        run_engine("pool", e)
        for sem, val in final_waits:
            e.wait_ge(sem, val)
        for ch in chans:
            e.wait_ge(chan_sems[ch], P.chan_count[ch])

    es.close()
    nops = {k: len(v) for k, v in per_eng.items()}
    return nc, nops


def fm(v, nchunk_axis_last=True):
    v = np.asarray(v, dtype=np.float32)
    lead = v.shape[:-1]
    n = v.shape[-1] // 128
    r = v.reshape(lead + (n, 128))
    r = np.moveaxis(r, -1, 0)
    return np.ascontiguousarray(r.reshape(128, -1))


def make_in_maps(cfg, inputs):
    c = cfg
    L = c.L
    common = {
        "ada_w": np.ascontiguousarray(inputs["ada_w"], dtype=np.float32),
        "ada_b": fm(inputs["ada_b"]),
        "g_mix": fm(inputs["mix_norm_g"]),
        "g_ffn": fm(inputs["ffn_norm_g"]),
        "g_fin": fm(inputs["final_norm_g"]),
        "w_in": np.ascontiguousarray(inputs["w_in"], dtype=np.float32),
        "b_fg": np.ascontiguousarray(np.asarray(inputs["b_forget"], dtype=np.float32).T),
        "cdw_w": np.ascontiguousarray(np.transpose(np.asarray(inputs["conf_dw_w"], np.float32).reshape(L, c.CW, c.NCC, 128), (3, 0, 2, 1)).reshape(128, -1)),
        "cdw_b": fm(inputs["conf_dw_b"]),
        "cln_g": fm(inputs["conf_ln_g"]),
        "cln_b": fm(inputs["conf_ln_b"]),
        "sdw_w": np.ascontiguousarray(np.transpose(np.asarray(inputs["sc_dw_w"], np.float32).reshape(L, c.SW, c.NSC, 128), (3, 0, 2, 1)).reshape(128, -1)),
        "w_out": np.ascontiguousarray(inputs["w_out"], dtype=np.float32),
        "w_up": np.ascontiguousarray(inputs["w_up"], dtype=np.float32),
        "fdw_w": np.ascontiguousarray(np.transpose(np.asarray(inputs["ffn_dw_w"], np.float32).reshape(L, c.FW, 2 * c.NFC, 128), (3, 0, 2, 1)).reshape(128, -1)),
        "fdw_b": fm(inputs["ffn_dw_b"]),
        "w_down": np.ascontiguousarray(inputs["w_down"], dtype=np.float32),
        "ident": np.eye(128, dtype=np.float32),
        "tri": np.triu(np.ones((128, 128), dtype=np.float32)),
    }
    x = np.asarray(inputs["x"], dtype=np.float32)
    cc = np.asarray(inputs["c"], dtype=np.float32)
    maps = []
    for b in range(c.ncores):
        m = dict(common)
        m["xT"] = np.ascontiguousarray(x[b].T)
        m["cT"] = fm(cc[b])
        maps.append(m)
    return maps


_CACHE = {}


def run_cfg(cfg, inputs, trace=False):
    key = (cfg.D, cfg.F, cfg.L, cfg.TOK, cfg.ncores)
    if key not in _CACHE:
        _CACHE[key] = build_program(cfg)
    nc, nops = _CACHE[key]
    maps = make_in_maps(cfg, inputs)
    res = run_bass_kernel_spmd(nc, maps, core_ids=list(range(cfg.ncores)), trace=trace)
    B = cfg.ncores
    out = np.stack([np.ascontiguousarray(res.results[b]["outT"].T) for b in range(B)], axis=0)
    return out.astype(np.float32), res


def kernel(**inputs):
    cfg = Cfg()
    out, _ = run_cfg(cfg, inputs)
    return out
```
